# Optimizing a Trainium2 kernel written in Bass

```python
import math
import jax, jax.numpy as jnp
from jax import lax
import numpy as np

D_MODEL = 2048
BATCH = 2
SEQ = 4096
DEPTH = 1

MIX_WIDTH = D_MODEL
POOL_WIDTH = MIX_WIDTH // 2
ATTN_WIDTH = MIX_WIDTH - POOL_WIDTH
POOL_WINDOWS = (2, 4, 8, 16)
N_POOL_GROUPS = len(POOL_WINDOWS)
POOL_GROUP_DIM = POOL_WIDTH // N_POOL_GROUPS
DIFF_HEAD_DIM = 64
DIFF_V_DIM = 2 * DIFF_HEAD_DIM
N_DIFF_HEADS = ATTN_WIDTH // DIFF_V_DIM
QK_WIDTH = N_DIFF_HEADS * 2 * DIFF_HEAD_DIM
IN_WIDTH = POOL_WIDTH + 2 * QK_WIDTH + ATTN_WIDTH
ROPE_THETA = 500000.0
ROT_DIM = DIFF_HEAD_DIM // 4
D_FF = int(math.ceil(8 * D_MODEL / 3 / 256) * 256)
Q_BLOCK = 128
NORM_EPS = 1e-6
NEG_INF = -1e30

kernel_name = "hybrid_pool_diffattn_block"


def rms_norm(x, g):
    xf = x.astype(jnp.float32)
    y = xf * lax.rsqrt(jnp.mean(xf * xf, axis=-1, keepdims=True) + NORM_EPS)
    return (y * g.astype(jnp.float32)).astype(x.dtype)


def lambda_init_fn(layer_idx):
    return 0.8 - 0.6 * math.exp(-0.3 * layer_idx)


def apply_partial_rope(t, positions):
    half = ROT_DIM // 2
    inv_freq = ROPE_THETA ** (-jnp.arange(0, ROT_DIM, 2, dtype=jnp.float32) / ROT_DIM)
    ang = positions.astype(jnp.float32)[..., None] * inv_freq
    cos = jnp.cos(ang)[:, :, None, None, :]
    sin = jnp.sin(ang)[:, :, None, None, :]
    tf = t.astype(jnp.float32)
    x1 = tf[..., :half]
    x2 = tf[..., half:ROT_DIM]
    rot = jnp.concatenate([x1 * cos - x2 * sin, x2 * cos + x1 * sin], axis=-1)
    return jnp.concatenate([rot, tf[..., ROT_DIM:]], axis=-1).astype(t.dtype)


def causal_multiscale_pool(u, pool_w, pool_scale):
    B, S, _ = u.shape
    ug = u.reshape(B, S, N_POOL_GROUPS, POOL_GROUP_DIM)
    ugf = ug.astype(jnp.float32)
    cs = jnp.cumsum(ugf, axis=1)
    t = jnp.arange(S)
    means = []
    for g, w in enumerate(POOL_WINDOWS):
        c = cs[:, :, g]
        prev = jnp.pad(c, ((0, 0), (w, 0), (0, 0)))[:, :S]
        cnt = jnp.minimum(t + 1, w).astype(jnp.float32)[None, :, None]
        means.append((c - prev) / cnt)
    mean = jnp.stack(means, axis=2)
    pooled = (mean - ugf).astype(u.dtype)
    mixed = jnp.einsum('bsgc,gcd->bsgd', pooled, pool_w)
    return mixed.reshape(B, S, POOL_WIDTH) * pool_scale


def differential_attention(q, k, v, lam):
    B, S = q.shape[0], q.shape[1]
    n_blocks = S // Q_BLOCK
    scale = DIFF_HEAD_DIM ** -0.5
    k_idx = jnp.arange(S)

    def block(i):
        start = i * Q_BLOCK
        qb = lax.dynamic_slice_in_dim(q, start, Q_BLOCK, axis=1)
        s = jnp.einsum('bqhcd,bkhcd->bhcqk', qb, k).astype(jnp.float32) * scale
        q_idx = start + jnp.arange(Q_BLOCK)
        mask = k_idx[None, :] <= q_idx[:, None]
        s = jnp.where(mask, s, NEG_INF)
        p = jax.nn.softmax(s, axis=-1)
        diff = p[:, :, 0] - lam * p[:, :, 1]
        return jnp.einsum('bhqk,bkhe->bqhe', diff.astype(v.dtype), v)

    out = lax.map(block, jnp.arange(n_blocks))
    out = jnp.transpose(out, (1, 0, 2, 3, 4)).reshape(B, S, N_DIFF_HEADS, DIFF_V_DIM)
    return out


def setup_inputs(seed: int = 0) -> dict:
    key = jax.random.key(seed)
    ks = jax.random.split(key, 20)
    f32 = jnp.float32

    def normal(k, shape, scale):
        return jax.random.normal(k, shape, f32) * scale

    def gain(k, shape):
        return 1.0 + 0.02 * jax.random.normal(k, shape, f32)

    x = jax.random.normal(ks[0], (BATCH, SEQ, D_MODEL), f32)
    positions = jnp.broadcast_to(jnp.arange(SEQ, dtype=jnp.int32)[None, :], (BATCH, SEQ))
    return {
        "x": x,
        "positions": positions,
        "pre_mix_norm": gain(ks[1], (DEPTH, D_MODEL)),
        "post_mix_norm": gain(ks[2], (DEPTH, D_MODEL)),
        "w_in": normal(ks[3], (DEPTH, D_MODEL, IN_WIDTH), D_MODEL ** -0.5),
        "pool_w": normal(ks[4], (DEPTH, N_POOL_GROUPS, POOL_GROUP_DIM, POOL_GROUP_DIM), POOL_GROUP_DIM ** -0.5),
        "pool_scale": gain(ks[5], (DEPTH, POOL_WIDTH)),
        "lam_q1": normal(ks[6], (DEPTH, DIFF_HEAD_DIM), 0.1),
        "lam_k1": normal(ks[7], (DEPTH, DIFF_HEAD_DIM), 0.1),
        "lam_q2": normal(ks[8], (DEPTH, DIFF_HEAD_DIM), 0.1),
        "lam_k2": normal(ks[9], (DEPTH, DIFF_HEAD_DIM), 0.1),
        "subln_w": gain(ks[10], (DEPTH, DIFF_V_DIM)),
        "w_out": normal(ks[11], (DEPTH, MIX_WIDTH, D_MODEL), MIX_WIDTH ** -0.5),
        "pre_ffn_norm": gain(ks[12], (DEPTH, D_MODEL)),
        "post_ffn_norm": gain(ks[13], (DEPTH, D_MODEL)),
        "w_gate": normal(ks[14], (DEPTH, D_MODEL, D_FF), D_MODEL ** -0.5),
        "w_up": normal(ks[15], (DEPTH, D_MODEL, D_FF), D_MODEL ** -0.5),
        "w_down": normal(ks[16], (DEPTH, D_FF, D_MODEL), D_FF ** -0.5),
    }


def reference(x, positions, pre_mix_norm, post_mix_norm, w_in, pool_w, pool_scale,
              lam_q1, lam_k1, lam_q2, lam_k2, subln_w, w_out,
              pre_ffn_norm, post_ffn_norm, w_gate, w_up, w_down):
    B, S, _ = x.shape
    h = x
    for l in range(DEPTH):
        lambda_init = lambda_init_fn(l)
        hn = rms_norm(h, pre_mix_norm[l])
        proj = hn @ w_in[l]
        u_pool = proj[..., :POOL_WIDTH]
        q = proj[..., POOL_WIDTH:POOL_WIDTH + QK_WIDTH].reshape(B, S, N_DIFF_HEADS, 2, DIFF_HEAD_DIM)
        k = proj[..., POOL_WIDTH + QK_WIDTH:POOL_WIDTH + 2 * QK_WIDTH].reshape(B, S, N_DIFF_HEADS, 2, DIFF_HEAD_DIM)
        v = proj[..., POOL_WIDTH + 2 * QK_WIDTH:].reshape(B, S, N_DIFF_HEADS, DIFF_V_DIM)

        pool_out = causal_multiscale_pool(u_pool, pool_w[l], pool_scale[l])

        q = apply_partial_rope(q, positions)
        k = apply_partial_rope(k, positions)
        lam = (jnp.exp(jnp.sum(lam_q1[l].astype(jnp.float32) * lam_k1[l].astype(jnp.float32)))
               - jnp.exp(jnp.sum(lam_q2[l].astype(jnp.float32) * lam_k2[l].astype(jnp.float32)))
               + lambda_init)
        attn = differential_attention(q, k, v, lam)
        attn = rms_norm(attn, subln_w[l]) * (1.0 - lambda_init)
        attn_out = attn.reshape(B, S, ATTN_WIDTH)

        mixed = jnp.concatenate([pool_out, attn_out], axis=-1) @ w_out[l]
        h = h + rms_norm(mixed, post_mix_norm[l])

        hn = rms_norm(h, pre_ffn_norm[l])
        ff = (jax.nn.silu(hn @ w_gate[l]) * (hn @ w_up[l])) @ w_down[l]
        h = h + rms_norm(ff, post_ffn_norm[l])
    return h
```

```python
import math
import numpy as np
import concourse.bass as bass
import concourse.mybir as mybir
from concourse.bass_utils import run_bass_kernel_spmd
from contextlib import ExitStack

F32 = mybir.dt.float32
BF16 = mybir.dt.bfloat16
I32 = mybir.dt.int32
AF = mybir.ActivationFunctionType
ALU = mybir.AluOpType
AX = mybir.AxisListType

D = 2048
S_LEN = 4096
NB = 32
DFF = 5632
NF = DFF // 128
EPS = 1e-6
LAMBDA_INIT = 0.8 - 0.6 * math.exp(-0.3 * 0)
NT = 41
ARENA_KB = 207


class Res:
    __slots__ = ("name", "w", "r", "dsem", "dcnt", "excl")

    def __init__(self, name, excl=False):
        self.name = name
        self.excl = excl
        self.w = None
        self.r = []
        self.dsem = None
        self.dcnt = 0


class Sched:
    ENG = ("pe", "act", "dve", "pool", "sp")

    def __init__(self, nc, stack):
        self.nc = nc
        self.stack = stack
        self.sem = {e: stack.enter_context(nc.semaphore("s_" + e)) for e in self.ENG}
        self.bar = stack.enter_context(nc.semaphore("s_bar"))
        self.barcnt = 0
        self.cnt = {e: 0 for e in self.ENG}
        self.waited = {e: {} for e in self.ENG}
        self.ops = {e: [] for e in self.ENG}
        self.dres = []
        self.nsem = 0

    def _deps(self, reads, writes, e=None):
        deps = []
        for r in reads:
            if r.w is not None:
                deps.append(r.w)
            if r.excl:
                own = self.sem.get(e)
                deps.extend(t for t in r.r if t[0] is not own)
        for w in writes:
            if w.w is not None:
                deps.append(w.w)
            deps.extend(w.r)
        return deps

    def _commit(self, e, fn, deps, tok, inc, reads, writes):
        waits = {}
        for (s, v) in deps:
            if e == "pe" and s is self.sem["pe"]:
                continue
            k = id(s)
            if self.waited[e].get(k, 0) >= v:
                continue
            if k not in waits or waits[k][1] < v:
                waits[k] = (s, v)
        for k, (s, v) in waits.items():
            self.waited[e][k] = v
        for r in reads:
            r.r.append(tok)
        for w in writes:
            w.w = tok
            w.r = []
        self.ops[e].append((fn, list(waits.values()), inc))

    def op(self, e, fn, reads=(), writes=(), signal=True):
        deps = self._deps(reads, writes, e)
        if signal:
            self.cnt[e] += 1
            tok = (self.sem[e], self.cnt[e])
            inc = (self.sem[e], 1)
        else:
            assert e == "pe"
            tok = (self.sem[e], self.cnt[e] + 1)
            inc = None
        self._commit(e, fn, deps, tok, inc, reads, writes)

    def dma(self, q, fn, reads=(), writes=(), group=None):
        deps = self._deps(reads, writes)
        res = writes[0] if group is None else group
        if res.dsem is None:
            res.dsem = self.stack.enter_context(self.nc.semaphore("d%d" % self.nsem))
            self.nsem += 1
            self.dres.append(res)
        res.dcnt += 16
        tok = (res.dsem, res.dcnt)
        self._commit(q, fn, deps, tok, (res.dsem, 16), reads, writes)

    def barrier(self):
        deps = [(self.sem[e], self.cnt[e]) for e in self.ENG if self.cnt[e] > 0]
        deps += [(r.dsem, r.dcnt) for r in self.dres]
        self.barcnt += 1
        bar = self.bar
        self._commit("sp", lambda eng: eng.sem_inc(bar, 1), deps, (bar, self.barcnt), None, (), ())
        for e in self.ENG:
            if e == "sp":
                continue
            self._commit(e, None, [(bar, self.barcnt)], None, None, (), ())

    def emit(self, finals=()):
        nc = self.nc
        engs = {"pe": "tensor", "act": "scalar", "dve": "vector", "pool": "gpsimd", "sp": "sync"}
        fin_waits = [r.w for r in finals if r.w is not None]
        with nc.Block() as block:
            for e in self.ENG:
                ops = self.ops[e]
                extra = fin_waits if e == "sp" else []

                def body(eng, ops=ops, extra=extra):
                    for (fn, waits, inc) in ops:
                        for (s, v) in waits:
                            eng.wait_ge(s, v)
                        if fn is None:
                            continue
                        ins = fn(eng)
                        if inc is not None:
                            ins.then_inc(inc[0], inc[1])
                    for (s, v) in extra:
                        eng.wait_ge(s, v)

                getattr(block, engs[e])(body)


class Arena:
    def __init__(self, ap):
        self.ap = ap
        self.off = 0
        self.limit = ap.shape[1] * 2

    def at(self, off):
        self.off = off

    def alloc(self, dtype, shape):
        esz = 4 if dtype in (F32, I32) else 2
        n = 1
        for s in shape:
            n *= s
        nbytes = n * esz
        off = (self.off + 63) // 64 * 64
        assert off + nbytes <= self.limit, ("arena overflow", off, nbytes, self.limit)
        self.off = off + nbytes
        v = self.ap[:, off // 2:(off + nbytes) // 2]
        if esz == 4:
            v = v.bitcast(dtype)
        if len(shape) == 2:
            return v.rearrange("p (a b) -> p a b", a=shape[0])
        if len(shape) == 3:
            return v.rearrange("p (a b c) -> p a b c", a=shape[0], b=shape[1])
        return v


def build_program(stop=None):
    nc = bass.Bass("TRN2", target_bir_lowering=False)

    def din(name, shape, dt=F32):
        return nc.dram_tensor(name, list(shape), dt, kind="ExternalInput").ap()

    xc = din("xc", [S_LEN, D])
    xo = din("xo", [9 * 128, D])
    pos_d = din("pos", [128, 40], I32)
    w_in = din("w_in", [D, 4096])
    pool_w = din("pool_w", [4, 256, 256])
    w_out = din("w_out", [D, D])
    w_gate = din("w_gate", [D, DFF])
    w_up = din("w_up", [D, DFF])
    w_down = din("w_down", [DFF, D])
    gpre_d = din("gpre", [128, 16])
    gffn_d = din("gffn", [128, 16])
    pscale_d = din("pscale", [128, 8])
    subw_d = din("subw", [128, 1])
    gpost_d = din("gpost", [1, D])
    gpostf_d = din("gpostf", [1, D])
    lam_d = din("lamv", [1, 256])
    ident_d = din("ident", [128, 128])
    mask_d = din("mask", [128, 8 * 2 * 128])
    am_d = din("amain", [128, 2 * 4 * 128])
    ah_d = din("ahalo", [128, 8 * 4 * 128])
    invf_d = din("invf", [128, 8])
    out_d = nc.dram_tensor("out", [8, 128, D], F32, kind="ExternalOutput").ap()
    dbgkind = "ExternalOutput" if stop is not None else "Internal"
    hn_d = nc.dram_tensor("hn_scr", [NT, 128, D], BF16, kind=dbgkind).ap()
    y_d = nc.dram_tensor("y_scr", [8, 128, D], F32, kind=dbgkind).ap()
    dbg_d = nc.dram_tensor("dbg", [128, 16 * 1024], BF16, kind="ExternalOutput").ap() if stop is not None else None

    st = ExitStack()
    with st:
        S = Sched(nc, st)
        arena_t = st.enter_context(nc.sbuf_tensor("arena", [128, ARENA_KB * 512], BF16))
        ps = st.enter_context(nc.psum_tensor("ps", [128, 4096], F32))
        A = Arena(arena_t[:])
        bankres = [Res("bank%d" % b, excl=True) for b in range(8)]

        def bank(b, n=1):
            return ps[:, b * 512:(b + n) * 512]

        def bank16(b):
            return ps[:, b * 512:(b + 1) * 512].bitcast(BF16)

        def MM(out, lhsT, rhs, start, stop, reads, writes, signal, **kw):
            S.op("pe", lambda eng: eng.matmul(out=out, lhsT=lhsT, rhs=rhs, start=start, stop=stop, **kw),
                 reads=reads, writes=writes, signal=signal)

        def TR(out, in_, reads, writes, signal):
            S.op("pe", lambda eng: eng.transpose(out=out, in_=in_, identity=ident), reads=reads, writes=writes, signal=signal)

        def ACTF(out, in_, func, reads, writes, **kw):
            S.op("act", lambda eng: eng.activation(out=out, in_=in_, func=func, **kw), reads=reads, writes=writes)

        def TT(e, out, in0, in1, op, reads, writes):
            S.op(e, lambda eng: eng.tensor_tensor(out=out, in0=in0, in1=in1, op=op), reads=reads, writes=writes)

        def TS(e, out, in0, s1, s2, op0, op1, reads, writes):
            if op1 is None:
                S.op(e, lambda eng: eng.tensor_scalar(out=out, in0=in0, scalar1=s1, scalar2=None, op0=op0), reads=reads, writes=writes)
            else:
                S.op(e, lambda eng: eng.tensor_scalar(out=out, in0=in0, scalar1=s1, scalar2=s2, op0=op0, op1=op1), reads=reads, writes=writes)

        def STT(out, in0, scalar, in1, op0, op1, reads, writes):
            S.op("dve", lambda eng: eng.scalar_tensor_tensor(out=out, in0=in0, scalar=scalar, in1=in1, op0=op0, op1=op1),
                 reads=reads, writes=writes)

        def CP(e, out, in_, reads, writes):
            if e == "act":
                S.op(e, lambda eng: eng.copy(out=out, in_=in_), reads=reads, writes=writes)
            else:
                S.op(e, lambda eng: eng.tensor_copy(out=out, in_=in_), reads=reads, writes=writes)

        def RED(out, in_, reads, writes):
            S.op("dve", lambda eng: eng.tensor_reduce(out=out, in_=in_, axis=AX.X, op=ALU.add), reads=reads, writes=writes)

        def RECIP(out, in_, reads, writes):
            S.op("dve", lambda eng: eng.reciprocal(out=out, in_=in_), reads=reads, writes=writes)

        def DMA(q, out, in_, reads, writes, group=None):
            S.dma(q, lambda eng: eng.dma_start(out=out, in_=in_), reads=reads, writes=writes, group=group)

        ident = A.alloc(BF16, [128])
        gpre = A.alloc(F32, [16])
        gffn = A.alloc(F32, [16])
        pscale = A.alloc(F32, [8])
        subw = A.alloc(F32, [4])
        lamv = A.alloc(F32, [256])
        lamtmp = A.alloc(F32, [128])
        lsc = A.alloc(F32, [8])
        invf = A.alloc(F32, [8])
        posi = A.alloc(I32, [40])
        posf = A.alloc(F32, [40])
        cosT = A.alloc(F32, [40, 8])
        sinT = A.alloc(F32, [40, 8])
        tr0 = A.alloc(F32, [40, 8])
        tr1 = A.alloc(F32, [40, 8])
        tr2 = A.alloc(F32, [40, 8])
        tri = A.alloc(I32, [40, 8])
        small = A.alloc(F32, [64])
        CONST_END = 12 * 1024
        assert A.off <= CONST_END, A.off
        rope_res = Res("rope")
        r_ident, r_lam, r_pos = Res("id"), Res("lam"), Res("pos")
        r_g1, r_g2, r_g3, r_g4, r_g5 = Res("g1"), Res("g2"), Res("g3"), Res("g4"), Res("g5")
        DMA("pool", ident, ident_d, [], [r_ident])
        DMA("sp", gpre, gpre_d, [], [r_g1])
        DMA("sp", gffn, gffn_d, [], [r_g2])
        DMA("sp", pscale, pscale_d, [], [r_g3])
        DMA("sp", subw[:, 0:1], subw_d, [], [r_g4])
        DMA("sp", invf, invf_d, [], [r_g5])
        DMA("sp", lamv, lam_d.broadcast_to([128, 256]), [], [r_lam])
        DMA("sp", posi, pos_d, [], [r_pos])

        r_lt, r_lsc = Res("lamtmp"), Res("lsc")
        TT("dve", lamtmp[:, 0:64], lamv[:, 0:64], lamv[:, 64:128], ALU.mult, [r_lam], [r_lt])
        TT("dve", lamtmp[:, 64:128], lamv[:, 128:192], lamv[:, 192:256], ALU.mult, [r_lam], [r_lt])
        RED(lsc[:, 0:2], lamtmp.rearrange("p (a b) -> p a b", a=2), [r_lt], [r_lsc])
        ACTF(lsc[:, 2:4], lsc[:, 0:2], AF.Exp, [r_lsc], [r_lsc])
        TT("dve", lsc[:, 4:5], lsc[:, 2:3], lsc[:, 3:4], ALU.subtract, [r_lsc], [r_lsc])
        TS("dve", lsc[:, 5:6], lsc[:, 4:5], float(LAMBDA_INIT), None, ALU.add, None, [r_lsc], [r_lsc])
        lam_ap = lsc[:, 5:6]
        TS("dve", subw[:, 1:2], subw[:, 0:1], float(1.0 - LAMBDA_INIT), None, ALU.mult, None, [r_g4], [r_g4])
        subw8 = subw[:, 1:2]

        TWO_PI = 2.0 * math.pi
        C1 = 6.28125
        C2 = TWO_PI - C1
        RR = [rope_res]
        CP("dve", posf, posi, [r_pos], RR)
        TT("dve", tr0, posf.unsqueeze(2).broadcast_to([128, 40, 8]), invf.unsqueeze(1).broadcast_to([128, 40, 8]), ALU.mult,
           [rope_res, r_g5], RR)

        def reduce_sin(dst, shift):
            TS("dve", tr1, tr0, float(shift), None, ALU.add, None, RR, RR)
            TS("dve", tr2, tr1, float(1.0 / TWO_PI), None, ALU.mult, None, RR, RR)
            CP("dve", tri, tr2, RR, RR)
            CP("dve", tr2, tri, RR, RR)
            STT(tr1, tr2, float(-C1), tr1, ALU.mult, ALU.add, RR, RR)
            STT(tr1, tr2, float(-C2), tr1, ALU.mult, ALU.add, RR, RR)
            TS("dve", tr2, tr1, float(math.pi), float(-TWO_PI), ALU.is_gt, ALU.mult, RR, RR)
            TT("dve", tr1, tr1, tr2, ALU.add, RR, RR)
            TS("dve", tr2, tr1, float(-math.pi), float(TWO_PI), ALU.is_lt, ALU.mult, RR, RR)
            TT("dve", tr1, tr1, tr2, ALU.add, RR, RR)
            TS("dve", tr1, tr1, 3.14159, -3.14159, ALU.min, ALU.max, RR, RR)
            ACTF(dst, tr1, AF.Sin, RR, RR)

        reduce_sin(sinT, 0.0)
        reduce_sin(cosT, math.pi / 2.0)

        def rstd_from_ss(ss_ap, rs_ap, n, reads, writes):
            TS("dve", rs_ap, ss_ap, float(1.0 / n), float(EPS), ALU.mult, ALU.add, reads, writes)
            ACTF(rs_ap, rs_ap, AF.Sqrt, writes, writes)
            RECIP(rs_ap, rs_ap, writes, writes)

        ssn = [small[:, k:k + 1] for k in range(4)]
        rsn = [small[:, 4 + k:5 + k] for k in range(4)]
        ss_r = [Res("ss%d" % k) for k in range(4)]

        def norm_transpose(src_ap, src_r, s, gain, b0, out_ap, out_r, sqb, sqb_r, xsb, xsb_r):
            ACTF(sqb, src_ap, AF.Square, [src_r], [sqb_r, ss_r[s]], accum_out=ssn[s])
            rstd_from_ss(ssn[s], rsn[s], D, [ss_r[s]], [ss_r[s]])
            ACTF(xsb, src_ap, AF.Copy, [src_r, ss_r[s]], [xsb_r], scale=rsn[s])
            for half in range(2):
                tb = bank16(b0 + half)
                for c8 in range(8):
                    c = half * 8 + c8
                    TR(tb[:, c8 * 128:(c8 + 1) * 128], xsb[:, c * 128:(c + 1) * 128], [xsb_r, r_ident], [bankres[b0 + half]], c8 == 7)
                TT("dve", out_ap[:, half * 8:(half + 1) * 8, :], tb.rearrange("p (c t) -> p c t", c=8),
                   gain[:, half * 8:(half + 1) * 8].unsqueeze(2).broadcast_to([128, 8, 128]), ALU.mult,
                   [bankres[b0 + half], r_g1, r_g2], [out_r])

        A.at(CONST_END)
        catT = A.alloc(BF16, [16, 1024])
        catres = [Res("cat%d" % c) for c in range(16)]
        PH_BASE = A.off

        A.at(PH_BASE)
        NS = 4
        xtN = [A.alloc(F32, [D]) for _ in range(NS)]
        sqN = [A.alloc(BF16, [D]) for _ in range(NS)]
        xsN = [A.alloc(BF16, [D]) for _ in range(NS)]
        hTN = [A.alloc(BF16, [16, 128]) for _ in range(NS)]
        xtN_r = [Res("xt%d" % i) for i in range(NS)]
        xsN_r = [Res("xs%d" % i) for i in range(NS)]
        hTN_r = [Res("hT%d" % i) for i in range(NS)]
        sqN_r = [Res("sq%d" % i) for i in range(NS)]
        hn_r = [Res("hn%d" % t) for t in range(NT)]
        g_hn = [Res("g_hn%d" % i) for i in range(NS)]
        g_y = [Res("g_y0"), Res("g_y1")]
        g_outB = [Res("g_oB0"), Res("g_oB1")]
        g_outE = [Res("g_oE%d" % i) for i in range(3)]
        for t in range(NT):
            s = t % NS
            src = xc[t * 128:(t + 1) * 128, :] if t < 32 else xo[(t - 32) * 128:(t - 31) * 128, :]
            DMA("sp", xtN[s], src, [], [xtN_r[s]])
            norm_transpose(xtN[s], xtN_r[s], s, gpre, (2 * t) % 8, hTN[s], hTN_r[s], sqN[s], sqN_r[s], xsN[s], xsN_r[s])
            DMA("sp", hn_d[t], hTN[s].rearrange("p c t -> p (c t)"), [hTN_r[s]], [hn_r[t]], group=g_hn[s])
        S.barrier()
        if stop == "N":
            S.emit()
            return nc

        A.at(PH_BASE)
        wp = A.alloc(BF16, [16, 1024])
        pw = A.alloc(BF16, [4, 2, 256])
        am = A.alloc(BF16, [2, 4, 128])
        ah = A.alloc(BF16, [8, 4, 128])
        hslP = [A.alloc(BF16, [16, 128]) for _ in range(2)]
        utm = A.alloc(BF16, [9, 1024])
        pooledT = A.alloc(BF16, [8, 1024])
        wp_r = [Res("wp0"), Res("wp1")]
        pw_r, am_r = Res("pw"), Res("am")
        ah_r = [Res("ah0"), Res("ah1")]
        hslP_r = [Res("hsl0"), Res("hsl1")]
        utm_r = [Res("utm%d" % i) for i in range(9)]
        pl_r = [Res("pl%d" % i) for i in range(8)]
        w_in_v = w_in.rearrange("(c p) n -> p c n", p=128)
        for n in range(2):
            DMA("pool", wp[:, :, n * 512:(n + 1) * 512], w_in_v[:, :, n * 512:(n + 1) * 512], [], [wp_r[n]])
        DMA("pool", pw, pool_w.rearrange("g (cc p) d -> p g cc d", p=128), [], [pw_r])
        DMA("pool", am.rearrange("p a b c -> p (a b c)"), am_d, [], [am_r])
        ahf = ah.rearrange("p a b c -> p (a b c)")
        for n in range(2):
            DMA("pool", ahf[:, n * 2048:(n + 1) * 2048], ah_d[:, n * 2048:(n + 1) * 2048], [], [ah_r[n]])
        bk = 0
        for i in range(9):
            s = i % 2
            DMA("sp", hslP[s].rearrange("p c t -> p (c t)"), hn_d[32 + i], [hn_r[32 + i]], [hslP_r[s]])
            for n in range(2):
                b = bk % 4
                bk += 1
                for c in range(16):
                    MM(bank(b), hslP[s][:, c, :], wp[:, c, n * 512:(n + 1) * 512], c == 0, c == 15,
                       [hslP_r[s], wp_r[n]], [bankres[b]], c == 15)
                CP("act", utm[:, i, n * 512:(n + 1) * 512], bank(b), [bankres[b]], [utm_r[i]])
        for i in range(8):
            var = 0 if i == 0 else 1
            b0 = 4 + 2 * (i % 2)
            for q in range(8):
                g = q // 2
                o_ap = ps[:, b0 * 512 + q * 128: b0 * 512 + (q + 1) * 128]
                br = bankres[b0 + q // 4]
                MM(o_ap, utm[:, i, q * 128:(q + 1) * 128], am[:, var, g, :], True, False, [utm_r[i], am_r], [br], False)
                MM(o_ap, utm[:, 8, q * 128:(q + 1) * 128], ah[:, i, g, :], False, True, [utm_r[8], ah_r[i // 4]], [br], q % 4 == 3)
            CP("dve", pooledT[:, :, i * 128:(i + 1) * 128], bank(b0, 2).rearrange("p (q t) -> p q t", q=8),
               [bankres[b0], bankres[b0 + 1]], [pl_r[i]])
        for g in range(4):
            for dd in range(2):
                for tt in range(2):
                    b = bk % 4
                    bk += 1
                    for cc in range(2):
                        MM(bank(b), pw[:, g, cc, dd * 128:(dd + 1) * 128], pooledT[:, g * 2 + cc, tt * 512:(tt + 1) * 512],
                           cc == 0, cc == 1, [pw_r] + pl_r[tt * 4:(tt + 1) * 4], [bankres[b]], cc == 1)
                    ACTF(catT[:, g * 2 + dd, tt * 512:(tt + 1) * 512], bank(b), AF.Copy, [bankres[b], r_g3], [catres[g * 2 + dd]],
                         scale=pscale[:, g * 2 + dd:g * 2 + dd + 1])
        S.barrier()
        if stop == "P":
            DMA("sp", dbg_d[:, 0:8192], catT[:, 0:8, :].rearrange("p c t -> p (c t)"), catres, [Res("dbgo")])
            S.barrier()
            S.emit()
            return nc

        A.at(PH_BASE)
        wk = A.alloc(BF16, [16, 512])
        wv = A.alloc(BF16, [16, 512])
        wq = A.alloc(BF16, [16, 512])
        kT = A.alloc(BF16, [4, S_LEN])
        vaug3 = A.alloc(BF16, [32 * 4, 132])
        vaug = vaug3.rearrange("p (t h) e -> p t h e", h=4)
        qTc = [A.alloc(BF16, [4, 1024]) for _ in range(2)]
        hslA = [A.alloc(BF16, [16, 128]) for _ in range(2)]
        ktm = [A.alloc(BF16, [8, 64]) for _ in range(2)]
        NPT = 3
        pt = [A.alloc(BF16, [1024]) for _ in range(NPT)]
        maskt = A.alloc(BF16, [8, 256])
        rt = [A.alloc(F32, [8, 8]) for _ in range(4)]
        ep_t1 = [A.alloc(F32, [128]) for _ in range(2)]
        ep_at = [A.alloc(F32, [128]) for _ in range(2)]
        ep_an = [A.alloc(BF16, [128]) for _ in range(2)]
        ep_sq = A.alloc(BF16, [128])
        ep_s = A.alloc(F32, [16])
        wk_r, wv_r, wq_r = Res("wk"), Res("wv"), Res("wq")
        k_r = [Res("k%d" % t) for t in range(32)]
        v_r = [Res("v%d" % t) for t in range(32)]
        q_r = [Res("q%d" % t) for t in range(8)]
        hslA_r = [Res("hslA0"), Res("hslA1")]
        ktm_r = [Res("ktm0"), Res("ktm1")]
        pt_r = [Res("pt%d" % i) for i in range(NPT)]
        mask_r = Res("mask")
        rt_r = Res("rt")
        ep_r = [Res("ep0"), Res("ep1")]
        epsq_r = Res("epsq")
        DMA("pool", maskt.rearrange("p a b -> p (a b)"), mask_d, [], [mask_r])
        S.op("dve", lambda eng: eng.memset(vaug3[:, :, 128:129], 1.0), writes=v_r)
        S.op("dve", lambda eng: eng.memset(vaug3[:, :, 129:132], 0.0), writes=v_r)
        S.op("dve", lambda eng: eng.memset(qTc[0].rearrange("p h t -> p (h t)"), 0.0), writes=q_r)
        S.op("dve", lambda eng: eng.memset(qTc[1].rearrange("p h t -> p (h t)"), 0.0), writes=q_r)

        def rope_to_ktm(b, s, t_idx):
            bv = bank(b).rearrange("p (g d) -> p g d", d=64)
            cs = cosT[:, t_idx, :].unsqueeze(1).broadcast_to([128, 8, 8])
            sn = sinT[:, t_idx, :].unsqueeze(1).broadcast_to([128, 8, 8])
            x1 = bv[:, :, 0:8]
            x2 = bv[:, :, 8:16]
            rd = [bankres[b], rope_res]
            TT("dve", rt[0], x1, cs, ALU.mult, rd, [rt_r])
            TT("dve", rt[1], x2, sn, ALU.mult, rd, [rt_r])
            TT("dve", rt[2], x2, cs, ALU.mult, rd, [rt_r])
            TT("dve", rt[3], x1, sn, ALU.mult, rd, [rt_r])
            CP("act", ktm[s][:, :, 16:64], bv[:, :, 16:64], [bankres[b]], [ktm_r[s]])
            TT("dve", ktm[s][:, :, 0:8], rt[0], rt[1], ALU.subtract, [rt_r], [ktm_r[s]])
            TT("dve", ktm[s][:, :, 8:16], rt[2], rt[3], ALU.add, [rt_r], [ktm_r[s]])

        def transpose_heads(s, tb_idx, dst_ap, dst_r, qtok=None):
            tb = bank16(tb_idx)
            km = ktm[s].rearrange("p g d -> p (g d)")
            for hh in range(4):
                TR(tb[:, hh * 128:(hh + 1) * 128], km[:, hh * 128:(hh + 1) * 128], [ktm_r[s], r_ident], [bankres[tb_idx]], hh == 3)
            tv = tb[:, 0:512].rearrange("p (h t) -> p h t", h=4)
            if qtok is None:
                CP("act", dst_ap, tv, [bankres[tb_idx]], [dst_r])
            else:
                for c in range(2):
                    CP("act", qTc[c][c * 64:(c + 1) * 64, :, qtok * 128:(qtok + 1) * 128], tv[c * 64:(c + 1) * 64], [bankres[tb_idx]], [dst_r])

        def attention(hg, hh, o, ob0):
            nkb = 8 * (o + 1)
            npair = nkb // 2
            qrd = [q_r[2 * o], q_r[2 * o + 1]]

            def s_mm(m):
                sb = 4 + 2 * (m % 2)
                for j in range(2):
                    kb = 2 * m + j
                    for c in range(2):
                        o_ap = ps[:, (sb + c) * 512 + j * 256:(sb + c) * 512 + (j + 1) * 256]
                        MM(o_ap, kT[:, hh, kb * 128:(kb + 1) * 128], qTc[c][:, hh, o * 256:(o + 1) * 256],
                           True, True, [k_r[kb]] + qrd, [bankres[sb + c]], True)

            def exp_mask(m):
                sb = 4 + 2 * (m % 2)
                p = m % NPT
                pv3 = pt[p].rearrange("p (c n) -> p c n", c=2)
                ACTF(pv3, bank(sb, 2).rearrange("p (c n) -> p c n", c=2), AF.Exp, [bankres[sb], bankres[sb + 1]], [pt_r[p]], scale=0.125)
                if 2 * m >= 8 * o:
                    r = 2 * m - 8 * o
                    mk = maskt[:, r:r + 2, :].rearrange("p a b -> p (a b)").unsqueeze(1).broadcast_to([128, 2, 512])
                    TT("dve", pv3, pv3, mk, ALU.mult, [pt_r[p], mask_r], [pt_r[p]])

            def pv_mm(m):
                p = m % NPT
                for j in range(2):
                    kb = 2 * m + j
                    for a in range(2):
                        for c in range(2):
                            o_ap = ps[:, (ob0 + a) * 512 + c * 130:(ob0 + a) * 512 + (c + 1) * 130]
                            MM(o_ap, pt[p][:, c * 512 + j * 256 + a * 128:c * 512 + j * 256 + (a + 1) * 128], vaug[:, kb, hh, 0:130],
                               (kb == 0 and c == 0), kb == nkb - 1, [pt_r[p], v_r[kb]], [bankres[ob0 + a]], c == 1,
                               skip_group_check=True)

            s_mm(0)
            exp_mask(0)
            for m in range(npair):
                if m + 1 < npair:
                    s_mm(m + 1)
                    exp_mask(m + 1)
                pv_mm(m)

            for a in range(2):
                e = a
                blk = 2 * o + a
                ob = bank(ob0 + a)
                br = bankres[ob0 + a]
                rz = ep_s[:, a * 8:a * 8 + 2]
                rzl = ep_s[:, a * 8 + 2:a * 8 + 3]
                ssq = ep_s[:, a * 8 + 3:a * 8 + 4]
                rsd = ep_s[:, a * 8 + 4:a * 8 + 5]
                ER = [ep_r[e]]
                RECIP(rz, ob[:, 128:259:130], [br], ER)
                TT("dve", rzl, rz[:, 1:2], lam_ap, ALU.mult, [ep_r[e], r_lsc], ER)
                TS("dve", ep_t1[e], ob[:, 130:258], rzl, None, ALU.mult, None, [br, ep_r[e]], ER)
                STT(ep_at[e], ob[:, 0:128], rz[:, 0:1], ep_t1[e], ALU.mult, ALU.subtract, [br, ep_r[e]], ER)
                ACTF(ep_sq, ep_at[e], AF.Square, ER, [ep_r[e], epsq_r], accum_out=ssq)
                rstd_from_ss(ssq, rsd, 128, ER, ER)
                ACTF(ep_an[e], ep_at[e], AF.Copy, ER, ER, scale=rsd)
                tbi = 2 + a
                tb = bank16(tbi)
                TR(tb[:, 0:128], ep_an[e], [ep_r[e], r_ident], [bankres[tbi]], True)
                ch = 8 + hg * 4 + hh
                ACTF(catT[:, ch, blk * 128:(blk + 1) * 128], tb[:, 0:128], AF.Copy, [bankres[tbi], r_g4], [catres[ch]], scale=subw8)

        tcount = 0
        pbk = 0
        attn_idx = 0
        for hg in range(2):
            DMA("pool", wk, w_in_v[:, :, 2048 + hg * 512: 2048 + (hg + 1) * 512], [], [wk_r])
            DMA("pool", wv, w_in_v[:, :, 3072 + hg * 512: 3072 + (hg + 1) * 512], [], [wv_r])
            DMA("pool", wq, w_in_v[:, :, 1024 + hg * 512: 1024 + (hg + 1) * 512], [], [wq_r])
            for t in range(32):
                s = tcount % 2
                tcount += 1
                DMA("sp", hslA[s].rearrange("p c t -> p (c t)"), hn_d[t], [hn_r[t]], [hslA_r[s]])
                bK = pbk % 4
                bV = (pbk + 1) % 4
                pbk += 2
                for c in range(16):
                    MM(bank(bK), hslA[s][:, c, :], wk[:, c, :], c == 0, c == 15, [hslA_r[s], wk_r], [bankres[bK]], c == 15)
                for c in range(16):
                    MM(bank(bV), hslA[s][:, c, :], wv[:, c, :], c == 0, c == 15, [hslA_r[s], wv_r], [bankres[bV]], c == 15)
                rope_to_ktm(bK, s, t)
                CP("act", vaug[:, t, :, 0:128], bank(bV).rearrange("p (h e) -> p h e", h=4), [bankres[bV]], [v_r[t]])
                transpose_heads(s, 4 + (t % 2), kT[:, :, t * 128:(t + 1) * 128], k_r[t])
            for i in range(8):
                s = tcount % 2
                tcount += 1
                DMA("sp", hslA[s].rearrange("p c t -> p (c t)"), hn_d[32 + i], [hn_r[32 + i]], [hslA_r[s]])
                bQ = pbk % 4
                pbk += 1
                for c in range(16):
                    MM(bank(bQ), hslA[s][:, c, :], wq[:, c, :], c == 0, c == 15, [hslA_r[s], wq_r], [bankres[bQ]], c == 15)
                rope_to_ktm(bQ, s, 32 + i)
                transpose_heads(s, 4 + (i % 2), None, q_r[i], qtok=i)
            for hh in range(4):
                for o in range(4):
                    attention(hg, hh, o, 0)
                    attn_idx += 1
        S.barrier()
        if stop == "A":
            DMA("sp", dbg_d, catT.rearrange("p c t -> p (c t)"), catres, [Res("dbgo")])
            S.barrier()
            S.emit()
            return nc

        A.at(PH_BASE)
        HN2_OFF = 175 * 1024
        wo = A.alloc(BF16, [16, D])
        gbc = A.alloc(F32, [D])
        xtB = [A.alloc(F32, [D]) for _ in range(2)]
        mB = [A.alloc(F32, [D]) for _ in range(2)]
        h1 = [A.alloc(F32, [D]) for _ in range(2)]
        sqB = A.alloc(BF16, [D])
        xsB0 = A.alloc(BF16, [D])
        xsB = [xsB0, xsB0]
        assert A.off <= HN2_OFF, A.off
        A.at(HN2_OFF)
        hn2T = A.alloc(BF16, [16, 1024])
        wo_r = [Res("wo%d" % n) for n in range(4)]
        gbc_r = Res("gbc")
        xtB_r = [Res("xtB%d" % i) for i in range(2)]
        mB_r = [Res("mB%d" % i) for i in range(2)]
        h1_r = [Res("h1%d" % i) for i in range(2)]
        xsB_r0 = Res("xsB")
        xsB_r = [xsB_r0, xsB_r0]
        sqB_r = Res("sqB")
        hn2_r = [Res("hn2_%d" % i) for i in range(8)]
        out_r = [Res("out%d" % i) for i in range(8)]
        ssB = small[:, 8:16]
        ssB_r = [Res("ssB0"), Res("ssB1")]
        sB = small[:, 16:24]
        w_out_v = w_out.rearrange("(c p) n -> p c n", p=128)
        for n in range(4):
            DMA("pool", wo[:, :, n * 512:(n + 1) * 512], w_out_v[:, :, n * 512:(n + 1) * 512], [], [wo_r[n]])
        DMA("sp", gbc, gpost_d.broadcast_to([128, D]), [], [gbc_r])
        mbk = 0
        for i in range(8):
            s = i % 2
            DMA("sp", xtB[s], xo[i * 128:(i + 1) * 128, :], [], [xtB_r[s]])
            for n in range(4):
                b = mbk % 6
                mbk += 1
                for c in range(16):
                    MM(bank(b), catT[:, c, i * 128:(i + 1) * 128], wo[:, c, n * 512:(n + 1) * 512], c == 0, c == 15,
                       [catres[c], wo_r[n]], [bankres[b]], c == 15)
                CP("dve", mB[s][:, n * 512:(n + 1) * 512], bank(b), [bankres[b]], [mB_r[s]])
                ACTF(sqB[:, n * 512:(n + 1) * 512], bank(b), AF.Square, [bankres[b]], [sqB_r, ssB_r[s]],
                     accum_out=ssB[:, s * 4 + n:s * 4 + n + 1])
            ssum = sB[:, s * 4:s * 4 + 1]
            rstd = sB[:, s * 4 + 1:s * 4 + 2]
            RED(ssum, ssB[:, s * 4:(s + 1) * 4], [ssB_r[s]], [ssB_r[s]])
            rstd_from_ss(ssum, rstd, D, [ssB_r[s]], [ssB_r[s]])
            STT(mB[s], mB[s], rstd, gbc, ALU.mult, ALU.mult, [mB_r[s], ssB_r[s], gbc_r], [mB_r[s]])
            TT("pool", h1[s], mB[s], xtB[s], ALU.add, [mB_r[s], xtB_r[s]], [h1_r[s]])
            DMA("sp", out_d[i], h1[s], [h1_r[s]], [out_r[i]], group=g_outB[s])
            norm_transpose(h1[s], h1_r[s], s, gffn, 6, hn2T[:, :, i * 128:(i + 1) * 128], hn2_r[i], sqB, sqB_r, xsB[s], xsB_r[s])
        S.barrier()
        if stop == "B":
            DMA("sp", dbg_d, hn2T.rearrange("p c t -> p (c t)"), hn2_r, [Res("dbgo")])
            S.barrier()
            S.emit()
            return nc

        A.at(CONST_END)
        actT = A.alloc(BF16, [NF, 1024])
        WD_OFF = A.off
        wg = [A.alloc(BF16, [16, 512]) for _ in range(2)]
        wu = [A.alloc(BF16, [16, 512]) for _ in range(2)]
        sg = [A.alloc(BF16, [512]) for _ in range(3)]
        assert A.off <= HN2_OFF, A.off
        wg_r = [Res("wg0"), Res("wg1")]
        wu_r = [Res("wu0"), Res("wu1")]
        sg_r = [Res("sg%d" % i) for i in range(3)]
        act_r = [Res("act%d" % f) for f in range(NF)]
        wg_v = w_gate.rearrange("(c p) n -> p c n", p=128)
        wu_v = w_up.rearrange("(c p) n -> p c n", p=128)
        gk = 0
        for f4 in range(NF // 4):
            s = f4 % 2
            DMA("pool", wg[s], wg_v[:, :, f4 * 512:(f4 + 1) * 512], [], [wg_r[s]])
            DMA("pool", wu[s], wu_v[:, :, f4 * 512:(f4 + 1) * 512], [], [wu_r[s]])
            for fi in range(4):
                f = f4 * 4 + fi
                for tt in range(2):
                    bg = (2 * gk) % 8
                    bu = bg + 1
                    k3 = gk % 3
                    gk += 1
                    hr = hn2_r[tt * 4:(tt + 1) * 4]
                    for c in range(16):
                        MM(bank(bg), wg[s][:, c, fi * 128:(fi + 1) * 128], hn2T[:, c, tt * 512:(tt + 1) * 512], c == 0, c == 15,
                           [wg_r[s]] + hr, [bankres[bg]], c == 15)
                    for c in range(16):
                        MM(bank(bu), wu[s][:, c, fi * 128:(fi + 1) * 128], hn2T[:, c, tt * 512:(tt + 1) * 512], c == 0, c == 15,
                           [wu_r[s]] + hr, [bankres[bu]], c == 15)
                    ACTF(sg[k3], bank(bg), AF.Silu, [bankres[bg]], [sg_r[k3]])
                    TT("dve", actT[:, f, tt * 512:(tt + 1) * 512], sg[k3], bank(bu), ALU.mult, [sg_r[k3], bankres[bu]], [act_r[f]])
        S.barrier()

        A.at(WD_OFF)
        wd = [A.alloc(BF16, [NF, 512]) for _ in range(2)]
        gbc2 = A.alloc(F32, [D])
        GBC2_END = A.off
        yst = [A.alloc(F32, [512]) for _ in range(2)]
        sqd = A.alloc(BF16, [512])
        wd_r = [[Res("wd%d_%d" % (s, k)) for k in range(4)] for s in range(2)]
        gbc2_r = Res("gbc2")
        yst_r = [Res("yst0"), Res("yst1")]
        sqd_r = Res("sqd")
        ssD = small[:, 24:56]
        ssD_r = [Res("ssD%d" % i) for i in range(8)]
        y_r = [Res("y%d" % i) for i in range(8)]
        wd_v = w_down.rearrange("(f p) n -> p f n", p=128)
        DMA("sp", gbc2, gpostf_d.broadcast_to([128, D]), [], [gbc2_r])
        dk = 0
        for n in range(4):
            s = n % 2
            for k in range(4):
                DMA("pool", wd[s][:, k * 11:(k + 1) * 11, :], wd_v[:, k * 11:(k + 1) * 11, n * 512:(n + 1) * 512], [], [wd_r[s][k]])
            for i in range(8):
                b = dk % 8
                ys = dk % 2
                dk += 1
                for f in range(NF):
                    MM(bank(b), actT[:, f, i * 128:(i + 1) * 128], wd[s][:, f, :], f == 0, f == NF - 1,
                       [act_r[f], wd_r[s][f // 11]], [bankres[b]], f == NF - 1)
                CP("dve", yst[ys], bank(b), [bankres[b]], [yst_r[ys]])
                ACTF(sqd, bank(b), AF.Square, [bankres[b]], [sqd_r, ssD_r[i]], accum_out=ssD[:, i * 4 + n:i * 4 + n + 1])
                DMA("sp", y_d[i, :, n * 512:(n + 1) * 512], yst[ys], [yst_r[ys]], [y_r[i]], group=g_y[ys])
        S.barrier()
        if stop == "D":
            S.emit()
            return nc

        A.at(WD_OFF)
        NE = 3
        yt = [A.alloc(F32, [D]) for _ in range(NE)]
        ht = [A.alloc(F32, [D]) for _ in range(NE)]
        ot = [A.alloc(F32, [D]) for _ in range(NE)]
        assert A.off <= GBC2_END - D * 4, A.off
        yt_r = [Res("yt%d" % i) for i in range(NE)]
        ht_r = [Res("ht%d" % i) for i in range(NE)]
        ot_r = [Res("ot%d" % i) for i in range(NE)]
        sE = A.alloc(F32, [16])
        for i in range(8):
            s = i % NE
            DMA("sp", yt[s], y_d[i], [y_r[i]], [yt_r[s]])
            DMA("sp", ht[s], out_d[i], [out_r[i]], [ht_r[s]])
            ssum = sE[:, s * 4:s * 4 + 1]
            rstd = sE[:, s * 4 + 1:s * 4 + 2]
            RED(ssum, ssD[:, i * 4:(i + 1) * 4], [ssD_r[i]], [ot_r[s]])
            rstd_from_ss(ssum, rstd, D, [ot_r[s]], [ot_r[s]])
            STT(ot[s], yt[s], rstd, gbc2, ALU.mult, ALU.mult, [yt_r[s], gbc2_r], [ot_r[s]])
            TT("pool", ot[s], ot[s], ht[s], ALU.add, [ht_r[s]], [ot_r[s]])
            DMA("sp", out_d[i], ot[s], [ot_r[s], ht_r[s]], [out_r[i]], group=g_outE[s])

        S.emit(finals=out_r)
    return nc


_PROG = None


def _own_blocks(j):
    ob = []
    for o in range(4):
        ob += [8 * o + j, 8 * o + 7 - j]
    return ob


def kernel(x, positions, pre_mix_norm, post_mix_norm, w_in, pool_w, pool_scale,
           lam_q1, lam_k1, lam_q2, lam_k2, subln_w, w_out,
           pre_ffn_norm, post_ffn_norm, w_gate, w_up, w_down):
    global _PROG
    if _PROG is None:
        _PROG = build_program()
    nc = _PROG
    in_maps, cores = prepare_inputs(x, positions, pre_mix_norm, post_mix_norm, w_in, pool_w, pool_scale,
                                    lam_q1, lam_k1, lam_q2, lam_k2, subln_w, w_out,
                                    pre_ffn_norm, post_ffn_norm, w_gate, w_up, w_down)
    res = run_bass_kernel_spmd(nc, in_maps, core_ids=list(range(8)))
    out = np.zeros((2, S_LEN, D), np.float32)
    for (b, ob), r in zip(cores, res.results):
        o = np.asarray(r["out"], np.float32).reshape(8, 128, D)
        for i, blk in enumerate(ob):
            out[b, blk * 128:(blk + 1) * 128] = o[i]
    return out


def prepare_inputs(x, positions, pre_mix_norm, post_mix_norm, w_in, pool_w, pool_scale,
                   lam_q1, lam_k1, lam_q2, lam_k2, subln_w, w_out,
                   pre_ffn_norm, post_ffn_norm, w_gate, w_up, w_down):
    f32 = np.float32
    x = np.asarray(x, f32)
    positions = np.asarray(positions, np.int32)

    def chunked(g, n):
        return np.ascontiguousarray(np.asarray(g, f32).reshape(n, 128).T)

    shared = {
        "w_in": np.ascontiguousarray(np.asarray(w_in, f32)[0]),
        "pool_w": np.ascontiguousarray(np.asarray(pool_w, f32)[0]),
        "w_out": np.ascontiguousarray(np.asarray(w_out, f32)[0]),
        "w_gate": np.ascontiguousarray(np.asarray(w_gate, f32)[0]),
        "w_up": np.ascontiguousarray(np.asarray(w_up, f32)[0]),
        "w_down": np.ascontiguousarray(np.asarray(w_down, f32)[0]),
        "gpre": chunked(pre_mix_norm[0], 16),
        "gffn": chunked(pre_ffn_norm[0], 16),
        "pscale": chunked(pool_scale[0], 8),
        "subw": np.ascontiguousarray(np.asarray(subln_w, f32)[0].reshape(128, 1)),
        "gpost": np.ascontiguousarray(np.asarray(post_mix_norm, f32)[0].reshape(1, D)),
        "gpostf": np.ascontiguousarray(np.asarray(post_ffn_norm, f32)[0].reshape(1, D)),
        "lamv": np.concatenate([np.asarray(v, f32)[0] for v in (lam_q1, lam_k1, lam_q2, lam_k2)]).reshape(1, 256),
        "ident": np.eye(128, dtype=f32),
    }
    i_half = np.arange(8, dtype=np.float64)
    invf = (500000.0 ** (-(2.0 * i_half) / 16.0)).astype(f32)
    shared["invf"] = np.ascontiguousarray(np.broadcast_to(invf[None, :], (128, 8)))
    s_idx = np.arange(128)
    windows = (2, 4, 8, 16)

    def band_main(first):
        m = np.zeros((128, 4, 128), f32)
        for g, w in enumerate(windows):
            for t in range(128):
                cnt = min(t + 1, w) if first else w
                lo = max(0, t - w + 1)
                m[lo:t + 1, g, t] += 1.0 / cnt
                m[t, g, t] -= 1.0
        return m

    def band_halo(i, zero):
        m = np.zeros((128, 4, 128), f32)
        if zero:
            return m
        for g, w in enumerate(windows):
            for t in range(128):
                for sabs in range(t - w + 1, 0):
                    r = 16 + sabs
                    m[16 * i + r, g, t] += 1.0 / w
        return m

    in_maps = []
    cores = []
    for b in range(2):
        for j in range(4):
            ob = _own_blocks(j)
            cores.append((b, ob))
            xo = np.zeros((9 * 128, D), f32)
            pos = np.zeros((128, 40), np.int32)
            pos[:, :32] = positions[b].reshape(32, 128).T
            for i, blk in enumerate(ob):
                xo[i * 128:(i + 1) * 128] = x[b, blk * 128:(blk + 1) * 128]
                pos[:, 32 + i] = positions[b, blk * 128:(blk + 1) * 128]
                if blk > 0:
                    xo[1024 + 16 * i:1024 + 16 * (i + 1)] = x[b, blk * 128 - 16:blk * 128]
            am = np.stack([band_main(ob[0] == 0), band_main(False)], axis=1)
            ah = np.stack([band_halo(i, ob[i] == 0) for i in range(8)], axis=1)
            mask = np.zeros((128, 8, 2, 128), f32)
            for a, jb in enumerate((j, 7 - j)):
                for r in range(8):
                    kk = r * 128 + s_idx[:, None]
                    qq = jb * 128 + s_idx[None, :]
                    mask[:, r, a, :] = (kk <= qq).astype(f32)
            m = dict(shared)
            m["xc"] = np.ascontiguousarray(x[b])
            m["xo"] = xo
            m["pos"] = pos
            m["amain"] = np.ascontiguousarray(am.reshape(128, -1))
            m["ahalo"] = np.ascontiguousarray(ah.reshape(128, -1))
            m["mask"] = np.ascontiguousarray(mask.reshape(128, -1))
            in_maps.append(m)
    return in_maps, cores
```

```python
import math
import numpy as np
import concourse.bass as bass
import concourse.mybir as mybir
from concourse.bass_utils import run_bass_kernel_spmd
from contextlib import ExitStack

F32 = mybir.dt.float32
BF16 = mybir.dt.bfloat16
I32 = mybir.dt.int32
AF = mybir.ActivationFunctionType
ALU = mybir.AluOpType
AX = mybir.AxisListType

D = 2048
S_LEN = 4096
NB = 32
DFF = 5632
NF = DFF // 128
EPS = 1e-6
LAMBDA_INIT = 0.8 - 0.6 * math.exp(-0.3 * 0)
NT = 41
ARENA_KB = 207


class Res:
    __slots__ = ("name", "w", "r", "dsem", "dcnt", "excl")

    def __init__(self, name, excl=False):
        self.name = name
        self.excl = excl
        self.w = None
        self.r = []
        self.dsem = None
        self.dcnt = 0


class Sched:
    ENG = ("pe", "act", "dve", "pool", "sp")

    def __init__(self, nc, stack):
        self.nc = nc
        self.stack = stack
        self.sem = {e: stack.enter_context(nc.semaphore("s_" + e)) for e in self.ENG}
        self.bar = stack.enter_context(nc.semaphore("s_bar"))
        self.barcnt = 0
        self.cnt = {e: 0 for e in self.ENG}
        self.waited = {e: {} for e in self.ENG}
        self.ops = {e: [] for e in self.ENG}
        self.dres = []
        self.nsem = 0

    def _deps(self, reads, writes, e=None):
        deps = []
        for r in reads:
            if r.w is not None:
                deps.append(r.w)
            if r.excl:
                own = self.sem.get(e)
                deps.extend(t for t in r.r if t[0] is not own)
        for w in writes:
            if w.w is not None:
                deps.append(w.w)
            deps.extend(w.r)
        return deps

    def _commit(self, e, fn, deps, tok, inc, reads, writes):
        waits = {}
        for (s, v) in deps:
            if e == "pe" and s is self.sem["pe"]:
                continue
            k = id(s)
            if self.waited[e].get(k, 0) >= v:
                continue
            if k not in waits or waits[k][1] < v:
                waits[k] = (s, v)
        for k, (s, v) in waits.items():
            self.waited[e][k] = v
        for r in reads:
            r.r.append(tok)
        for w in writes:
            w.w = tok
            w.r = []
        self.ops[e].append((fn, list(waits.values()), inc))

    def op(self, e, fn, reads=(), writes=(), signal=True):
        deps = self._deps(reads, writes, e)
        if signal:
            self.cnt[e] += 1
            tok = (self.sem[e], self.cnt[e])
            inc = (self.sem[e], 1)
        else:
            assert e == "pe"
            tok = (self.sem[e], self.cnt[e] + 1)
            inc = None
        self._commit(e, fn, deps, tok, inc, reads, writes)

    def dma(self, q, fn, reads=(), writes=(), group=None):
        deps = self._deps(reads, writes)
        res = writes[0] if group is None else group
        if res.dsem is None:
            res.dsem = self.stack.enter_context(self.nc.semaphore("d%d" % self.nsem))
            self.nsem += 1
            self.dres.append(res)
        res.dcnt += 16
        tok = (res.dsem, res.dcnt)
        self._commit(q, fn, deps, tok, (res.dsem, 16), reads, writes)

    def barrier(self):
        deps = [(self.sem[e], self.cnt[e]) for e in self.ENG if self.cnt[e] > 0]
        deps += [(r.dsem, r.dcnt) for r in self.dres]
        self.barcnt += 1
        bar = self.bar
        self._commit("sp", lambda eng: eng.sem_inc(bar, 1), deps, (bar, self.barcnt), None, (), ())
        for e in self.ENG:
            if e == "sp":
                continue
            self._commit(e, None, [(bar, self.barcnt)], None, None, (), ())

    def emit(self, finals=()):
        nc = self.nc
        engs = {"pe": "tensor", "act": "scalar", "dve": "vector", "pool": "gpsimd", "sp": "sync"}
        fin_waits = [r.w for r in finals if r.w is not None]
        with nc.Block() as block:
            for e in self.ENG:
                ops = self.ops[e]
                extra = fin_waits if e == "sp" else []

                def body(eng, ops=ops, extra=extra):
                    for (fn, waits, inc) in ops:
                        for (s, v) in waits:
                            eng.wait_ge(s, v)
                        if fn is None:
                            continue
                        ins = fn(eng)
                        if inc is not None:
                            ins.then_inc(inc[0], inc[1])
                    for (s, v) in extra:
                        eng.wait_ge(s, v)

                getattr(block, engs[e])(body)


class Arena:
    def __init__(self, ap):
        self.ap = ap
        self.off = 0
        self.limit = ap.shape[1] * 2

    def at(self, off):
        self.off = off

    def alloc(self, dtype, shape):
        esz = 4 if dtype in (F32, I32) else 2
        n = 1
        for s in shape:
            n *= s
        nbytes = n * esz
        off = (self.off + 63) // 64 * 64
        assert off + nbytes <= self.limit, ("arena overflow", off, nbytes, self.limit)
        self.off = off + nbytes
        v = self.ap[:, off // 2:(off + nbytes) // 2]
        if esz == 4:
            v = v.bitcast(dtype)
        if len(shape) == 2:
            return v.rearrange("p (a b) -> p a b", a=shape[0])
        if len(shape) == 3:
            return v.rearrange("p (a b c) -> p a b c", a=shape[0], b=shape[1])
        return v


def build_program(stop=None):
    nc = bass.Bass("TRN2", target_bir_lowering=False)

    def din(name, shape, dt=F32):
        return nc.dram_tensor(name, list(shape), dt, kind="ExternalInput").ap()

    xc = din("xc", [S_LEN, D])
    xo = din("xo", [9 * 128, D])
    pos_d = din("pos", [128, 40], I32)
    w_in = din("w_in", [D, 4096])
    pool_w = din("pool_w", [4, 256, 256])
    w_out = din("w_out", [D, D])
    w_gate = din("w_gate", [D, DFF])
    w_up = din("w_up", [D, DFF])
    w_down = din("w_down", [DFF, D])
    gpre_d = din("gpre", [128, 16])
    gffn_d = din("gffn", [128, 16])
    pscale_d = din("pscale", [128, 8])
    subw_d = din("subw", [128, 1])
    gpost_d = din("gpost", [1, D])
    gpostf_d = din("gpostf", [1, D])
    lam_d = din("lamv", [1, 256])
    ident_d = din("ident", [128, 128])
    mask_d = din("mask", [128, 8 * 2 * 128])
    am_d = din("amain", [128, 2 * 4 * 128])
    ah_d = din("ahalo", [128, 8 * 4 * 128])
    invf_d = din("invf", [128, 8])
    out_d = nc.dram_tensor("out", [8, 128, D], F32, kind="ExternalOutput").ap()
    dbgkind = "ExternalOutput" if stop is not None else "Internal"
    hn_d = nc.dram_tensor("hn_scr", [NT, 128, D], BF16, kind=dbgkind).ap()
    y_d = nc.dram_tensor("y_scr", [8, 128, D], F32, kind=dbgkind).ap()
    dbg_d = nc.dram_tensor("dbg", [128, 16 * 1024], BF16, kind="ExternalOutput").ap() if stop is not None else None

    st = ExitStack()
    with st:
        S = Sched(nc, st)
        arena_t = st.enter_context(nc.sbuf_tensor("arena", [128, ARENA_KB * 512], BF16))
        ps = st.enter_context(nc.psum_tensor("ps", [128, 4096], F32))
        A = Arena(arena_t[:])
        bankres = [Res("bank%d" % b, excl=True) for b in range(8)]

        def bank(b, n=1):
            return ps[:, b * 512:(b + n) * 512]

        def bank16(b):
            return ps[:, b * 512:(b + 1) * 512].bitcast(BF16)

        def MM(out, lhsT, rhs, start, stop, reads, writes, signal, **kw):
            S.op("pe", lambda eng: eng.matmul(out=out, lhsT=lhsT, rhs=rhs, start=start, stop=stop, **kw),
                 reads=reads, writes=writes, signal=signal)

        def TR(out, in_, reads, writes, signal):
            S.op("pe", lambda eng: eng.transpose(out=out, in_=in_, identity=ident), reads=reads, writes=writes, signal=signal)

        def ACTF(out, in_, func, reads, writes, **kw):
            S.op("act", lambda eng: eng.activation(out=out, in_=in_, func=func, **kw), reads=reads, writes=writes)

        def TT(e, out, in0, in1, op, reads, writes):
            S.op(e, lambda eng: eng.tensor_tensor(out=out, in0=in0, in1=in1, op=op), reads=reads, writes=writes)

        def TS(e, out, in0, s1, s2, op0, op1, reads, writes):
            if op1 is None:
                S.op(e, lambda eng: eng.tensor_scalar(out=out, in0=in0, scalar1=s1, scalar2=None, op0=op0), reads=reads, writes=writes)
            else:
                S.op(e, lambda eng: eng.tensor_scalar(out=out, in0=in0, scalar1=s1, scalar2=s2, op0=op0, op1=op1), reads=reads, writes=writes)

        def STT(out, in0, scalar, in1, op0, op1, reads, writes):
            S.op("dve", lambda eng: eng.scalar_tensor_tensor(out=out, in0=in0, scalar=scalar, in1=in1, op0=op0, op1=op1),
                 reads=reads, writes=writes)

        def CP(e, out, in_, reads, writes):
            if e == "act":
                S.op(e, lambda eng: eng.copy(out=out, in_=in_), reads=reads, writes=writes)
            else:
                S.op(e, lambda eng: eng.tensor_copy(out=out, in_=in_), reads=reads, writes=writes)

        def RED(out, in_, reads, writes):
            S.op("dve", lambda eng: eng.tensor_reduce(out=out, in_=in_, axis=AX.X, op=ALU.add), reads=reads, writes=writes)

        def RECIP(out, in_, reads, writes):
            S.op("dve", lambda eng: eng.reciprocal(out=out, in_=in_), reads=reads, writes=writes)

        def DMA(q, out, in_, reads, writes, group=None):
            S.dma(q, lambda eng: eng.dma_start(out=out, in_=in_), reads=reads, writes=writes, group=group)

        ident = A.alloc(BF16, [128])
        gpre = A.alloc(F32, [16])
        gffn = A.alloc(F32, [16])
        pscale = A.alloc(F32, [8])
        subw = A.alloc(F32, [4])
        lamv = A.alloc(F32, [256])
        lamtmp = A.alloc(F32, [128])
        lsc = A.alloc(F32, [8])
        invf = A.alloc(F32, [8])
        posi = A.alloc(I32, [40])
        posf = A.alloc(F32, [40])
        cosT = A.alloc(F32, [40, 8])
        sinT = A.alloc(F32, [40, 8])
        tr0 = A.alloc(F32, [40, 8])
        tr1 = A.alloc(F32, [40, 8])
        tr2 = A.alloc(F32, [40, 8])
        tri = A.alloc(I32, [40, 8])
        small = A.alloc(F32, [64])
        CONST_END = 12 * 1024
        assert A.off <= CONST_END, A.off
        rope_res = Res("rope")
        r_ident, r_lam, r_pos = Res("id"), Res("lam"), Res("pos")
        r_g1, r_g2, r_g3, r_g4, r_g5 = Res("g1"), Res("g2"), Res("g3"), Res("g4"), Res("g5")
        DMA("pool", ident, ident_d, [], [r_ident])
        DMA("sp", gpre, gpre_d, [], [r_g1])
        DMA("sp", gffn, gffn_d, [], [r_g2])
        DMA("sp", pscale, pscale_d, [], [r_g3])
        DMA("sp", subw[:, 0:1], subw_d, [], [r_g4])
        DMA("sp", invf, invf_d, [], [r_g5])
        DMA("sp", lamv, lam_d.broadcast_to([128, 256]), [], [r_lam])
        DMA("sp", posi, pos_d, [], [r_pos])

        r_lt, r_lsc = Res("lamtmp"), Res("lsc")
        TT("dve", lamtmp[:, 0:64], lamv[:, 0:64], lamv[:, 64:128], ALU.mult, [r_lam], [r_lt])
        TT("dve", lamtmp[:, 64:128], lamv[:, 128:192], lamv[:, 192:256], ALU.mult, [r_lam], [r_lt])
        RED(lsc[:, 0:2], lamtmp.rearrange("p (a b) -> p a b", a=2), [r_lt], [r_lsc])
        ACTF(lsc[:, 2:4], lsc[:, 0:2], AF.Exp, [r_lsc], [r_lsc])
        TT("dve", lsc[:, 4:5], lsc[:, 2:3], lsc[:, 3:4], ALU.subtract, [r_lsc], [r_lsc])
        TS("dve", lsc[:, 5:6], lsc[:, 4:5], float(LAMBDA_INIT), None, ALU.add, None, [r_lsc], [r_lsc])
        lam_ap = lsc[:, 5:6]
        TS("dve", subw[:, 1:2], subw[:, 0:1], float(1.0 - LAMBDA_INIT), None, ALU.mult, None, [r_g4], [r_g4])
        subw8 = subw[:, 1:2]

        TWO_PI = 2.0 * math.pi
        C1 = 6.28125
        C2 = TWO_PI - C1
        RR = [rope_res]
        CP("dve", posf, posi, [r_pos], RR)
        TT("dve", tr0, posf.unsqueeze(2).broadcast_to([128, 40, 8]), invf.unsqueeze(1).broadcast_to([128, 40, 8]), ALU.mult,
           [rope_res, r_g5], RR)

        def reduce_sin(dst, shift):
            TS("dve", tr1, tr0, float(shift), None, ALU.add, None, RR, RR)
            TS("dve", tr2, tr1, float(1.0 / TWO_PI), None, ALU.mult, None, RR, RR)
            CP("dve", tri, tr2, RR, RR)
            CP("dve", tr2, tri, RR, RR)
            STT(tr1, tr2, float(-C1), tr1, ALU.mult, ALU.add, RR, RR)
            STT(tr1, tr2, float(-C2), tr1, ALU.mult, ALU.add, RR, RR)
            TS("dve", tr2, tr1, float(math.pi), float(-TWO_PI), ALU.is_gt, ALU.mult, RR, RR)
            TT("dve", tr1, tr1, tr2, ALU.add, RR, RR)
            TS("dve", tr2, tr1, float(-math.pi), float(TWO_PI), ALU.is_lt, ALU.mult, RR, RR)
            TT("dve", tr1, tr1, tr2, ALU.add, RR, RR)
            TS("dve", tr1, tr1, 3.14159, -3.14159, ALU.min, ALU.max, RR, RR)
            ACTF(dst, tr1, AF.Sin, RR, RR)

        reduce_sin(sinT, 0.0)
        reduce_sin(cosT, math.pi / 2.0)

        def rstd_from_ss(ss_ap, rs_ap, n, reads, writes):
            TS("dve", rs_ap, ss_ap, float(1.0 / n), float(EPS), ALU.mult, ALU.add, reads, writes)
            ACTF(rs_ap, rs_ap, AF.Sqrt, writes, writes)
            RECIP(rs_ap, rs_ap, writes, writes)

        ssn = [small[:, k:k + 1] for k in range(4)]
        rsn = [small[:, 4 + k:5 + k] for k in range(4)]
        ss_r = [Res("ss%d" % k) for k in range(4)]

        def norm_transpose(src_ap, src_r, s, gain, b0, out_ap, out_r, sqb, sqb_r, xsb, xsb_r):
            ACTF(sqb, src_ap, AF.Square, [src_r], [sqb_r, ss_r[s]], accum_out=ssn[s])
            rstd_from_ss(ssn[s], rsn[s], D, [ss_r[s]], [ss_r[s]])
            ACTF(xsb, src_ap, AF.Copy, [src_r, ss_r[s]], [xsb_r], scale=rsn[s])
            for half in range(2):
                tb = bank16(b0 + half)
                for c8 in range(8):
                    c = half * 8 + c8
                    TR(tb[:, c8 * 128:(c8 + 1) * 128], xsb[:, c * 128:(c + 1) * 128], [xsb_r, r_ident], [bankres[b0 + half]], c8 == 7)
                TT("dve", out_ap[:, half * 8:(half + 1) * 8, :], tb.rearrange("p (c t) -> p c t", c=8),
                   gain[:, half * 8:(half + 1) * 8].unsqueeze(2).broadcast_to([128, 8, 128]), ALU.mult,
                   [bankres[b0 + half], r_g1, r_g2], [out_r])

        A.at(CONST_END)
        catT = A.alloc(BF16, [16, 1024])
        catres = [Res("cat%d" % c) for c in range(16)]
        PH_BASE = A.off

        A.at(PH_BASE)
        NS = 4
        xtN = [A.alloc(F32, [D]) for _ in range(NS)]
        sqN = [A.alloc(BF16, [D]) for _ in range(NS)]
        xsN = [A.alloc(BF16, [D]) for _ in range(NS)]
        hTN = [A.alloc(BF16, [16, 128]) for _ in range(NS)]
        xtN_r = [Res("xt%d" % i) for i in range(NS)]
        xsN_r = [Res("xs%d" % i) for i in range(NS)]
        hTN_r = [Res("hT%d" % i) for i in range(NS)]
        sqN_r = [Res("sq%d" % i) for i in range(NS)]
        hn_r = [Res("hn%d" % t) for t in range(NT)]
        g_hn = [Res("g_hn%d" % i) for i in range(NS)]
        g_y = [Res("g_y0"), Res("g_y1")]
        g_outB = [Res("g_oB0"), Res("g_oB1")]
        g_outE = [Res("g_oE%d" % i) for i in range(3)]
        for t in range(NT):
            s = t % NS
            src = xc[t * 128:(t + 1) * 128, :] if t < 32 else xo[(t - 32) * 128:(t - 31) * 128, :]
            DMA("sp", xtN[s], src, [], [xtN_r[s]])
            norm_transpose(xtN[s], xtN_r[s], s, gpre, (2 * t) % 8, hTN[s], hTN_r[s], sqN[s], sqN_r[s], xsN[s], xsN_r[s])
            DMA("pool", hn_d[t], hTN[s].rearrange("p c t -> p (c t)"), [hTN_r[s]], [hn_r[t]], group=g_hn[s])
        S.barrier()
        if stop == "N":
            S.emit()
            return nc

        A.at(PH_BASE)
        wp = A.alloc(BF16, [16, 1024])
        pw = A.alloc(BF16, [4, 2, 256])
        am = A.alloc(BF16, [2, 4, 128])
        ah = A.alloc(BF16, [8, 4, 128])
        hslP = [A.alloc(BF16, [16, 128]) for _ in range(2)]
        utm = A.alloc(BF16, [9, 1024])
        pooledT = A.alloc(BF16, [8, 1024])
        wp_r = [Res("wp0"), Res("wp1")]
        pw_r, am_r = Res("pw"), Res("am")
        ah_r = [Res("ah0"), Res("ah1")]
        hslP_r = [Res("hsl0"), Res("hsl1")]
        utm_r = [Res("utm%d" % i) for i in range(9)]
        pl_r = [Res("pl%d" % i) for i in range(8)]
        w_in_v = w_in.rearrange("(c p) n -> p c n", p=128)
        for n in range(2):
            DMA("pool", wp[:, :, n * 512:(n + 1) * 512], w_in_v[:, :, n * 512:(n + 1) * 512], [], [wp_r[n]])
        DMA("pool", pw, pool_w.rearrange("g (cc p) d -> p g cc d", p=128), [], [pw_r])
        DMA("pool", am.rearrange("p a b c -> p (a b c)"), am_d, [], [am_r])
        ahf = ah.rearrange("p a b c -> p (a b c)")
        for n in range(2):
            DMA("pool", ahf[:, n * 2048:(n + 1) * 2048], ah_d[:, n * 2048:(n + 1) * 2048], [], [ah_r[n]])
        bk = 0
        for i in range(9):
            s = i % 2
            DMA("sp", hslP[s].rearrange("p c t -> p (c t)"), hn_d[32 + i], [hn_r[32 + i]], [hslP_r[s]])
            for n in range(2):
                b = bk % 4
                bk += 1
                for c in range(16):
                    MM(bank(b), hslP[s][:, c, :], wp[:, c, n * 512:(n + 1) * 512], c == 0, c == 15,
                       [hslP_r[s], wp_r[n]], [bankres[b]], c == 15)
                CP("act", utm[:, i, n * 512:(n + 1) * 512], bank(b), [bankres[b]], [utm_r[i]])
        for i in range(8):
            var = 0 if i == 0 else 1
            b0 = 4 + 2 * (i % 2)
            for q in range(8):
                g = q // 2
                o_ap = ps[:, b0 * 512 + q * 128: b0 * 512 + (q + 1) * 128]
                br = bankres[b0 + q // 4]
                MM(o_ap, utm[:, i, q * 128:(q + 1) * 128], am[:, var, g, :], True, False, [utm_r[i], am_r], [br], False)
                MM(o_ap, utm[:, 8, q * 128:(q + 1) * 128], ah[:, i, g, :], False, True, [utm_r[8], ah_r[i // 4]], [br], q % 4 == 3)
            CP("dve", pooledT[:, :, i * 128:(i + 1) * 128], bank(b0, 2).rearrange("p (q t) -> p q t", q=8),
               [bankres[b0], bankres[b0 + 1]], [pl_r[i]])
        for g in range(4):
            for dd in range(2):
                for tt in range(2):
                    b = bk % 4
                    bk += 1
                    for cc in range(2):
                        MM(bank(b), pw[:, g, cc, dd * 128:(dd + 1) * 128], pooledT[:, g * 2 + cc, tt * 512:(tt + 1) * 512],
                           cc == 0, cc == 1, [pw_r] + pl_r[tt * 4:(tt + 1) * 4], [bankres[b]], cc == 1)
                    ACTF(catT[:, g * 2 + dd, tt * 512:(tt + 1) * 512], bank(b), AF.Copy, [bankres[b], r_g3], [catres[g * 2 + dd]],
                         scale=pscale[:, g * 2 + dd:g * 2 + dd + 1])
        S.barrier()
        if stop == "P":
            DMA("sp", dbg_d[:, 0:8192], catT[:, 0:8, :].rearrange("p c t -> p (c t)"), catres, [Res("dbgo")])
            S.barrier()
            S.emit()
            return nc

        A.at(PH_BASE)
        wk = A.alloc(BF16, [16, 512])
        wv = A.alloc(BF16, [16, 512])
        wq = A.alloc(BF16, [16, 512])
        kT = A.alloc(BF16, [4, S_LEN])
        vaug3 = A.alloc(BF16, [32 * 4, 132])
        vaug = vaug3.rearrange("p (t h) e -> p t h e", h=4)
        qTc = [A.alloc(BF16, [4, 1024]) for _ in range(2)]
        hslA = [A.alloc(BF16, [16, 128]) for _ in range(2)]
        ktm = [A.alloc(BF16, [8, 64]) for _ in range(2)]
        NPT = 3
        pt = [A.alloc(BF16, [1024]) for _ in range(NPT)]
        maskt = A.alloc(BF16, [8, 256])
        rt = [A.alloc(F32, [8, 8]) for _ in range(4)]
        ep_t1 = [A.alloc(F32, [128]) for _ in range(2)]
        ep_at = [A.alloc(F32, [128]) for _ in range(2)]
        ep_an = [A.alloc(BF16, [128]) for _ in range(2)]
        ep_sq = A.alloc(BF16, [128])
        ep_s = A.alloc(F32, [16])
        wk_r, wv_r, wq_r = Res("wk"), Res("wv"), Res("wq")
        k_r = [Res("k%d" % t) for t in range(32)]
        v_r = [Res("v%d" % t) for t in range(32)]
        q_r = [Res("q%d" % t) for t in range(8)]
        hslA_r = [Res("hslA0"), Res("hslA1")]
        ktm_r = [Res("ktm0"), Res("ktm1")]
        pt_r = [Res("pt%d" % i) for i in range(NPT)]
        mask_r = Res("mask")
        rt_r = Res("rt")
        ep_r = [Res("ep0"), Res("ep1")]
        epsq_r = Res("epsq")
        DMA("pool", maskt.rearrange("p a b -> p (a b)"), mask_d, [], [mask_r])
        S.op("dve", lambda eng: eng.memset(vaug3[:, :, 128:129], 1.0), writes=v_r)
        S.op("dve", lambda eng: eng.memset(vaug3[:, :, 129:132], 0.0), writes=v_r)
        S.op("dve", lambda eng: eng.memset(qTc[0].rearrange("p h t -> p (h t)"), 0.0), writes=q_r)
        S.op("dve", lambda eng: eng.memset(qTc[1].rearrange("p h t -> p (h t)"), 0.0), writes=q_r)

        def rope_to_ktm(b, s, t_idx):
            bv = bank(b).rearrange("p (g d) -> p g d", d=64)
            cs = cosT[:, t_idx, :].unsqueeze(1).broadcast_to([128, 8, 8])
            sn = sinT[:, t_idx, :].unsqueeze(1).broadcast_to([128, 8, 8])
            x1 = bv[:, :, 0:8]
            x2 = bv[:, :, 8:16]
            rd = [bankres[b], rope_res]
            TT("dve", rt[0], x1, cs, ALU.mult, rd, [rt_r])
            TT("dve", rt[1], x2, sn, ALU.mult, rd, [rt_r])
            TT("dve", rt[2], x2, cs, ALU.mult, rd, [rt_r])
            TT("dve", rt[3], x1, sn, ALU.mult, rd, [rt_r])
            CP("act", ktm[s][:, :, 16:64], bv[:, :, 16:64], [bankres[b]], [ktm_r[s]])
            TT("dve", ktm[s][:, :, 0:8], rt[0], rt[1], ALU.subtract, [rt_r], [ktm_r[s]])
            TT("dve", ktm[s][:, :, 8:16], rt[2], rt[3], ALU.add, [rt_r], [ktm_r[s]])

        def transpose_heads(s, tb_idx, dst_ap, dst_r, qtok=None):
            tb = bank16(tb_idx)
            km = ktm[s].rearrange("p g d -> p (g d)")
            for hh in range(4):
                TR(tb[:, hh * 128:(hh + 1) * 128], km[:, hh * 128:(hh + 1) * 128], [ktm_r[s], r_ident], [bankres[tb_idx]], hh == 3)
            tv = tb[:, 0:512].rearrange("p (h t) -> p h t", h=4)
            if qtok is None:
                CP("act", dst_ap, tv, [bankres[tb_idx]], [dst_r])
            else:
                for c in range(2):
                    CP("act", qTc[c][c * 64:(c + 1) * 64, :, qtok * 128:(qtok + 1) * 128], tv[c * 64:(c + 1) * 64], [bankres[tb_idx]], [dst_r])

        def attention(hg, hh, o, ob0):
            nkb = 8 * (o + 1)
            npair = nkb // 2
            qrd = [q_r[2 * o], q_r[2 * o + 1]]

            def s_mm(m):
                sb = 4 + 2 * (m % 2)
                for j in range(2):
                    kb = 2 * m + j
                    for c in range(2):
                        o_ap = ps[:, (sb + c) * 512 + j * 256:(sb + c) * 512 + (j + 1) * 256]
                        MM(o_ap, kT[:, hh, kb * 128:(kb + 1) * 128], qTc[c][:, hh, o * 256:(o + 1) * 256],
                           True, True, [k_r[kb]] + qrd, [bankres[sb + c]], True)

            def exp_mask(m):
                sb = 4 + 2 * (m % 2)
                p = m % NPT
                pv3 = pt[p].rearrange("p (c n) -> p c n", c=2)
                ACTF(pv3, bank(sb, 2).rearrange("p (c n) -> p c n", c=2), AF.Exp, [bankres[sb], bankres[sb + 1]], [pt_r[p]], scale=0.125)
                if 2 * m >= 8 * o:
                    r = 2 * m - 8 * o
                    mk = maskt[:, r:r + 2, :].rearrange("p a b -> p (a b)").unsqueeze(1).broadcast_to([128, 2, 512])
                    TT("dve", pv3, pv3, mk, ALU.mult, [pt_r[p], mask_r], [pt_r[p]])

            def pv_mm(m):
                p = m % NPT
                for j in range(2):
                    kb = 2 * m + j
                    for a in range(2):
                        for c in range(2):
                            o_ap = ps[:, (ob0 + a) * 512 + c * 130:(ob0 + a) * 512 + (c + 1) * 130]
                            MM(o_ap, pt[p][:, c * 512 + j * 256 + a * 128:c * 512 + j * 256 + (a + 1) * 128], vaug[:, kb, hh, 0:130],
                               (kb == 0 and c == 0), kb == nkb - 1, [pt_r[p], v_r[kb]], [bankres[ob0 + a]], c == 1,
                               skip_group_check=True)

            s_mm(0)
            exp_mask(0)
            for m in range(npair):
                if m + 1 < npair:
                    s_mm(m + 1)
                    exp_mask(m + 1)
                pv_mm(m)

            for a in range(2):
                e = a
                blk = 2 * o + a
                ob = bank(ob0 + a)
                br = bankres[ob0 + a]
                rz = ep_s[:, a * 8:a * 8 + 2]
                rzl = ep_s[:, a * 8 + 2:a * 8 + 3]
                ssq = ep_s[:, a * 8 + 3:a * 8 + 4]
                rsd = ep_s[:, a * 8 + 4:a * 8 + 5]
                ER = [ep_r[e]]
                RECIP(rz, ob[:, 128:259:130], [br], ER)
                TT("dve", rzl, rz[:, 1:2], lam_ap, ALU.mult, [ep_r[e], r_lsc], ER)
                TS("dve", ep_t1[e], ob[:, 130:258], rzl, None, ALU.mult, None, [br, ep_r[e]], ER)
                STT(ep_at[e], ob[:, 0:128], rz[:, 0:1], ep_t1[e], ALU.mult, ALU.subtract, [br, ep_r[e]], ER)
                ACTF(ep_sq, ep_at[e], AF.Square, ER, [ep_r[e], epsq_r], accum_out=ssq)
                rstd_from_ss(ssq, rsd, 128, ER, ER)
                ACTF(ep_an[e], ep_at[e], AF.Copy, ER, ER, scale=rsd)
                tbi = 2 + a
                tb = bank16(tbi)
                TR(tb[:, 0:128], ep_an[e], [ep_r[e], r_ident], [bankres[tbi]], True)
                ch = 8 + hg * 4 + hh
                ACTF(catT[:, ch, blk * 128:(blk + 1) * 128], tb[:, 0:128], AF.Copy, [bankres[tbi], r_g4], [catres[ch]], scale=subw8)

        tcount = 0
        pbk = 0
        attn_idx = 0
        for hg in range(2):
            DMA("pool", wk, w_in_v[:, :, 2048 + hg * 512: 2048 + (hg + 1) * 512], [], [wk_r])
            DMA("pool", wv, w_in_v[:, :, 3072 + hg * 512: 3072 + (hg + 1) * 512], [], [wv_r])
            DMA("pool", wq, w_in_v[:, :, 1024 + hg * 512: 1024 + (hg + 1) * 512], [], [wq_r])
            for t in range(32):
                s = tcount % 2
                tcount += 1
                DMA("sp", hslA[s].rearrange("p c t -> p (c t)"), hn_d[t], [hn_r[t]], [hslA_r[s]])
                bK = pbk % 4
                bV = (pbk + 1) % 4
                pbk += 2
                for c in range(16):
                    MM(bank(bK), hslA[s][:, c, :], wk[:, c, :], c == 0, c == 15, [hslA_r[s], wk_r], [bankres[bK]], c == 15)
                for c in range(16):
                    MM(bank(bV), hslA[s][:, c, :], wv[:, c, :], c == 0, c == 15, [hslA_r[s], wv_r], [bankres[bV]], c == 15)
                rope_to_ktm(bK, s, t)
                CP("act", vaug[:, t, :, 0:128], bank(bV).rearrange("p (h e) -> p h e", h=4), [bankres[bV]], [v_r[t]])
                transpose_heads(s, 4 + (t % 2), kT[:, :, t * 128:(t + 1) * 128], k_r[t])
            for i in range(8):
                s = tcount % 2
                tcount += 1
                DMA("sp", hslA[s].rearrange("p c t -> p (c t)"), hn_d[32 + i], [hn_r[32 + i]], [hslA_r[s]])
                bQ = pbk % 4
                pbk += 1
                for c in range(16):
                    MM(bank(bQ), hslA[s][:, c, :], wq[:, c, :], c == 0, c == 15, [hslA_r[s], wq_r], [bankres[bQ]], c == 15)
                rope_to_ktm(bQ, s, 32 + i)
                transpose_heads(s, 4 + (i % 2), None, q_r[i], qtok=i)
            for hh in range(4):
                for o in range(4):
                    attention(hg, hh, o, 0)
                    attn_idx += 1
        S.barrier()
        if stop == "A":
            DMA("sp", dbg_d, catT.rearrange("p c t -> p (c t)"), catres, [Res("dbgo")])
            S.barrier()
            S.emit()
            return nc

        A.at(PH_BASE)
        HN2_OFF = 175 * 1024
        wo = A.alloc(BF16, [16, D])
        gbc = A.alloc(F32, [D])
        xtB = [A.alloc(F32, [D]) for _ in range(2)]
        mB = [A.alloc(F32, [D]) for _ in range(2)]
        h1 = [A.alloc(F32, [D]) for _ in range(2)]
        sqB = A.alloc(BF16, [D])
        xsB0 = A.alloc(BF16, [D])
        xsB = [xsB0, xsB0]
        assert A.off <= HN2_OFF, A.off
        A.at(HN2_OFF)
        hn2T = A.alloc(BF16, [16, 1024])
        wo_r = [Res("wo%d" % n) for n in range(4)]
        gbc_r = Res("gbc")
        xtB_r = [Res("xtB%d" % i) for i in range(2)]
        mB_r = [Res("mB%d" % i) for i in range(2)]
        h1_r = [Res("h1%d" % i) for i in range(2)]
        xsB_r0 = Res("xsB")
        xsB_r = [xsB_r0, xsB_r0]
        sqB_r = Res("sqB")
        hn2_r = [Res("hn2_%d" % i) for i in range(8)]
        out_r = [Res("out%d" % i) for i in range(8)]
        ssB = small[:, 8:16]
        ssB_r = [Res("ssB0"), Res("ssB1")]
        sB = small[:, 16:24]
        w_out_v = w_out.rearrange("(c p) n -> p c n", p=128)
        for n in range(4):
            DMA("pool", wo[:, :, n * 512:(n + 1) * 512], w_out_v[:, :, n * 512:(n + 1) * 512], [], [wo_r[n]])
        DMA("sp", gbc, gpost_d.broadcast_to([128, D]), [], [gbc_r])
        mbk = 0
        for i in range(8):
            s = i % 2
            DMA("sp", xtB[s], xo[i * 128:(i + 1) * 128, :], [], [xtB_r[s]])
            for n in range(4):
                b = mbk % 6
                mbk += 1
                for c in range(16):
                    MM(bank(b), catT[:, c, i * 128:(i + 1) * 128], wo[:, c, n * 512:(n + 1) * 512], c == 0, c == 15,
                       [catres[c], wo_r[n]], [bankres[b]], c == 15)
                CP("dve", mB[s][:, n * 512:(n + 1) * 512], bank(b), [bankres[b]], [mB_r[s]])
                ACTF(sqB[:, n * 512:(n + 1) * 512], bank(b), AF.Square, [bankres[b]], [sqB_r, ssB_r[s]],
                     accum_out=ssB[:, s * 4 + n:s * 4 + n + 1])
            ssum = sB[:, s * 4:s * 4 + 1]
            rstd = sB[:, s * 4 + 1:s * 4 + 2]
            RED(ssum, ssB[:, s * 4:(s + 1) * 4], [ssB_r[s]], [ssB_r[s]])
            rstd_from_ss(ssum, rstd, D, [ssB_r[s]], [ssB_r[s]])
            STT(mB[s], mB[s], rstd, gbc, ALU.mult, ALU.mult, [mB_r[s], ssB_r[s], gbc_r], [mB_r[s]])
            TT("pool", h1[s], mB[s], xtB[s], ALU.add, [mB_r[s], xtB_r[s]], [h1_r[s]])
            DMA("pool", out_d[i], h1[s], [h1_r[s]], [out_r[i]], group=g_outB[s])
            norm_transpose(h1[s], h1_r[s], s, gffn, 6, hn2T[:, :, i * 128:(i + 1) * 128], hn2_r[i], sqB, sqB_r, xsB[s], xsB_r[s])
        S.barrier()
        if stop == "B":
            DMA("sp", dbg_d, hn2T.rearrange("p c t -> p (c t)"), hn2_r, [Res("dbgo")])
            S.barrier()
            S.emit()
            return nc

        A.at(CONST_END)
        actT = A.alloc(BF16, [NF, 1024])
        WD_OFF = A.off
        wg = [A.alloc(BF16, [16, 512]) for _ in range(2)]
        wu = [A.alloc(BF16, [16, 512]) for _ in range(2)]
        sg = [A.alloc(BF16, [512]) for _ in range(3)]
        assert A.off <= HN2_OFF, A.off
        wg_r = [Res("wg0"), Res("wg1")]
        wu_r = [Res("wu0"), Res("wu1")]
        sg_r = [Res("sg%d" % i) for i in range(3)]
        act_r = [Res("act%d" % f) for f in range(NF)]
        wg_v = w_gate.rearrange("(c p) n -> p c n", p=128)
        wu_v = w_up.rearrange("(c p) n -> p c n", p=128)
        gk = 0
        for f4 in range(NF // 4):
            s = f4 % 2
            DMA("pool", wg[s], wg_v[:, :, f4 * 512:(f4 + 1) * 512], [], [wg_r[s]])
            DMA("pool", wu[s], wu_v[:, :, f4 * 512:(f4 + 1) * 512], [], [wu_r[s]])
            for fi in range(4):
                f = f4 * 4 + fi
                for tt in range(2):
                    bg = (2 * gk) % 8
                    bu = bg + 1
                    k3 = gk % 3
                    gk += 1
                    hr = hn2_r[tt * 4:(tt + 1) * 4]
                    for c in range(16):
                        MM(bank(bg), wg[s][:, c, fi * 128:(fi + 1) * 128], hn2T[:, c, tt * 512:(tt + 1) * 512], c == 0, c == 15,
                           [wg_r[s]] + hr, [bankres[bg]], c == 15)
                    for c in range(16):
                        MM(bank(bu), wu[s][:, c, fi * 128:(fi + 1) * 128], hn2T[:, c, tt * 512:(tt + 1) * 512], c == 0, c == 15,
                           [wu_r[s]] + hr, [bankres[bu]], c == 15)
                    ACTF(sg[k3], bank(bg), AF.Silu, [bankres[bg]], [sg_r[k3]])
                    TT("dve", actT[:, f, tt * 512:(tt + 1) * 512], sg[k3], bank(bu), ALU.mult, [sg_r[k3], bankres[bu]], [act_r[f]])
        S.barrier()

        A.at(WD_OFF)
        wd = [A.alloc(BF16, [NF, 512]) for _ in range(2)]
        gbc2 = A.alloc(F32, [D])
        GBC2_END = A.off
        yst = [A.alloc(F32, [512]) for _ in range(2)]
        sqd = A.alloc(BF16, [512])
        wd_r = [[Res("wd%d_%d" % (s, k)) for k in range(4)] for s in range(2)]
        gbc2_r = Res("gbc2")
        yst_r = [Res("yst0"), Res("yst1")]
        sqd_r = Res("sqd")
        ssD = small[:, 24:56]
        ssD_r = [Res("ssD%d" % i) for i in range(8)]
        y_r = [Res("y%d" % i) for i in range(8)]
        wd_v = w_down.rearrange("(f p) n -> p f n", p=128)
        DMA("sp", gbc2, gpostf_d.broadcast_to([128, D]), [], [gbc2_r])
        dk = 0
        for n in range(4):
            s = n % 2
            for k in range(4):
                DMA("pool", wd[s][:, k * 11:(k + 1) * 11, :], wd_v[:, k * 11:(k + 1) * 11, n * 512:(n + 1) * 512], [], [wd_r[s][k]])
            for i in range(8):
                b = dk % 8
                ys = dk % 2
                dk += 1
                for f in range(NF):
                    MM(bank(b), actT[:, f, i * 128:(i + 1) * 128], wd[s][:, f, :], f == 0, f == NF - 1,
                       [act_r[f], wd_r[s][f // 11]], [bankres[b]], f == NF - 1)
                CP("dve", yst[ys], bank(b), [bankres[b]], [yst_r[ys]])
                ACTF(sqd, bank(b), AF.Square, [bankres[b]], [sqd_r, ssD_r[i]], accum_out=ssD[:, i * 4 + n:i * 4 + n + 1])
                DMA("sp", y_d[i, :, n * 512:(n + 1) * 512], yst[ys], [yst_r[ys]], [y_r[i]], group=g_y[ys])
        S.barrier()
        if stop == "D":
            S.emit()
            return nc

        A.at(WD_OFF)
        NE = 3
        yt = [A.alloc(F32, [D]) for _ in range(NE)]
        ht = [A.alloc(F32, [D]) for _ in range(NE)]
        ot = [A.alloc(F32, [D]) for _ in range(NE)]
        assert A.off <= GBC2_END - D * 4, A.off
        yt_r = [Res("yt%d" % i) for i in range(NE)]
        ht_r = [Res("ht%d" % i) for i in range(NE)]
        ot_r = [Res("ot%d" % i) for i in range(NE)]
        sE = A.alloc(F32, [16])
        for i in range(8):
            s = i % NE
            DMA("sp", yt[s], y_d[i], [y_r[i]], [yt_r[s]])
            DMA("sp", ht[s], out_d[i], [out_r[i]], [ht_r[s]])
            ssum = sE[:, s * 4:s * 4 + 1]
            rstd = sE[:, s * 4 + 1:s * 4 + 2]
            RED(ssum, ssD[:, i * 4:(i + 1) * 4], [ssD_r[i]], [ot_r[s]])
            rstd_from_ss(ssum, rstd, D, [ot_r[s]], [ot_r[s]])
            STT(ot[s], yt[s], rstd, gbc2, ALU.mult, ALU.mult, [yt_r[s], gbc2_r], [ot_r[s]])
            TT("pool", ot[s], ot[s], ht[s], ALU.add, [ht_r[s]], [ot_r[s]])
            DMA("pool", out_d[i], ot[s], [ot_r[s], ht_r[s]], [out_r[i]], group=g_outE[s])

        S.emit(finals=out_r)
    return nc


_PROG = None


def _own_blocks(j):
    ob = []
    for o in range(4):
        ob += [8 * o + j, 8 * o + 7 - j]
    return ob


def kernel(x, positions, pre_mix_norm, post_mix_norm, w_in, pool_w, pool_scale,
           lam_q1, lam_k1, lam_q2, lam_k2, subln_w, w_out,
           pre_ffn_norm, post_ffn_norm, w_gate, w_up, w_down):
    global _PROG
    if _PROG is None:
        _PROG = build_program()
    nc = _PROG
    in_maps, cores = prepare_inputs(x, positions, pre_mix_norm, post_mix_norm, w_in, pool_w, pool_scale,
                                    lam_q1, lam_k1, lam_q2, lam_k2, subln_w, w_out,
                                    pre_ffn_norm, post_ffn_norm, w_gate, w_up, w_down)
    res = run_bass_kernel_spmd(nc, in_maps, core_ids=list(range(8)))
    out = np.zeros((2, S_LEN, D), np.float32)
    for (b, ob), r in zip(cores, res.results):
        o = np.asarray(r["out"], np.float32).reshape(8, 128, D)
        for i, blk in enumerate(ob):
            out[b, blk * 128:(blk + 1) * 128] = o[i]
    return out


def prepare_inputs(x, positions, pre_mix_norm, post_mix_norm, w_in, pool_w, pool_scale,
                   lam_q1, lam_k1, lam_q2, lam_k2, subln_w, w_out,
                   pre_ffn_norm, post_ffn_norm, w_gate, w_up, w_down):
    f32 = np.float32
    x = np.asarray(x, f32)
    positions = np.asarray(positions, np.int32)

    def chunked(g, n):
        return np.ascontiguousarray(np.asarray(g, f32).reshape(n, 128).T)

    shared = {
        "w_in": np.ascontiguousarray(np.asarray(w_in, f32)[0]),
        "pool_w": np.ascontiguousarray(np.asarray(pool_w, f32)[0]),
        "w_out": np.ascontiguousarray(np.asarray(w_out, f32)[0]),
        "w_gate": np.ascontiguousarray(np.asarray(w_gate, f32)[0]),
        "w_up": np.ascontiguousarray(np.asarray(w_up, f32)[0]),
        "w_down": np.ascontiguousarray(np.asarray(w_down, f32)[0]),
        "gpre": chunked(pre_mix_norm[0], 16),
        "gffn": chunked(pre_ffn_norm[0], 16),
        "pscale": chunked(pool_scale[0], 8),
        "subw": np.ascontiguousarray(np.asarray(subln_w, f32)[0].reshape(128, 1)),
        "gpost": np.ascontiguousarray(np.asarray(post_mix_norm, f32)[0].reshape(1, D)),
        "gpostf": np.ascontiguousarray(np.asarray(post_ffn_norm, f32)[0].reshape(1, D)),
        "lamv": np.concatenate([np.asarray(v, f32)[0] for v in (lam_q1, lam_k1, lam_q2, lam_k2)]).reshape(1, 256),
        "ident": np.eye(128, dtype=f32),
    }
    i_half = np.arange(8, dtype=np.float64)
    invf = (500000.0 ** (-(2.0 * i_half) / 16.0)).astype(f32)
    shared["invf"] = np.ascontiguousarray(np.broadcast_to(invf[None, :], (128, 8)))
    s_idx = np.arange(128)
    windows = (2, 4, 8, 16)

    def band_main(first):
        m = np.zeros((128, 4, 128), f32)
        for g, w in enumerate(windows):
            for t in range(128):
                cnt = min(t + 1, w) if first else w
                lo = max(0, t - w + 1)
                m[lo:t + 1, g, t] += 1.0 / cnt
                m[t, g, t] -= 1.0
        return m

    def band_halo(i, zero):
        m = np.zeros((128, 4, 128), f32)
        if zero:
            return m
        for g, w in enumerate(windows):
            for t in range(128):
                for sabs in range(t - w + 1, 0):
                    r = 16 + sabs
                    m[16 * i + r, g, t] += 1.0 / w
        return m

    in_maps = []
    cores = []
    for b in range(2):
        for j in range(4):
            ob = _own_blocks(j)
            cores.append((b, ob))
            xo = np.zeros((9 * 128, D), f32)
            pos = np.zeros((128, 40), np.int32)
            pos[:, :32] = positions[b].reshape(32, 128).T
            for i, blk in enumerate(ob):
                xo[i * 128:(i + 1) * 128] = x[b, blk * 128:(blk + 1) * 128]
                pos[:, 32 + i] = positions[b, blk * 128:(blk + 1) * 128]
                if blk > 0:
                    xo[1024 + 16 * i:1024 + 16 * (i + 1)] = x[b, blk * 128 - 16:blk * 128]
            am = np.stack([band_main(ob[0] == 0), band_main(False)], axis=1)
            ah = np.stack([band_halo(i, ob[i] == 0) for i in range(8)], axis=1)
            mask = np.zeros((128, 8, 2, 128), f32)
            for a, jb in enumerate((j, 7 - j)):
                for r in range(8):
                    kk = r * 128 + s_idx[:, None]
                    qq = jb * 128 + s_idx[None, :]
                    mask[:, r, a, :] = (kk <= qq).astype(f32)
            m = dict(shared)
            m["xc"] = np.ascontiguousarray(x[b])
            m["xo"] = xo
            m["pos"] = pos
            m["amain"] = np.ascontiguousarray(am.reshape(128, -1))
            m["ahalo"] = np.ascontiguousarray(ah.reshape(128, -1))
            m["mask"] = np.ascontiguousarray(mask.reshape(128, -1))
            in_maps.append(m)
    return in_maps, cores
```
